# Optimizing a Trainium2 kernel written in Bass

```python
import jax, jax.numpy as jnp
from jax import lax
import numpy as np

D_MODEL = 1024
BATCH = 1
SEQ = 16384
DEPTH = 1
DEC_BATCH = 32
DEC_SEQ = 16
PAST_LEN = 4096

CHUNK = 64
N_META = 16
RET_WIDTH = D_MODEL // 2
RET_HEADS = 4
HEAD_DIM = RET_WIDTH // RET_HEADS
CONV_WIDTH = D_MODEL - RET_WIDTH
CONV_K = 31
D_FF = 4 * D_MODEL
N_IN = 4 * RET_WIDTH + 2 * CONV_WIDTH
EPS = 1e-6
ROPE_BASE = 10000.0

kernel_name = "hymba_retention_conformer_stream"


def rms_norm(x, g):
    xf = x.astype(jnp.float32)
    y = xf * lax.rsqrt(jnp.mean(xf * xf, axis=-1, keepdims=True) + EPS)
    return (y * g.astype(jnp.float32)).astype(x.dtype)


def layer_norm(x, g, b):
    xf = x.astype(jnp.float32)
    mu = jnp.mean(xf, axis=-1, keepdims=True)
    var = jnp.mean(jnp.square(xf - mu), axis=-1, keepdims=True)
    y = (xf - mu) * lax.rsqrt(var + EPS)
    return (y * g.astype(jnp.float32) + b.astype(jnp.float32)).astype(x.dtype)


def rope(x, pos):
    half = HEAD_DIM // 2
    inv = ROPE_BASE ** (-jnp.arange(half, dtype=jnp.float32) / half)
    ang = pos.astype(jnp.float32)[:, None] * inv[None, :]
    cos = jnp.cos(ang)[None, :, None, :]
    sin = jnp.sin(ang)[None, :, None, :]
    xf = x.astype(jnp.float32)
    x1, x2 = xf[..., :half], xf[..., half:]
    return jnp.concatenate([x1 * cos - x2 * sin, x2 * cos + x1 * sin], axis=-1).astype(x.dtype)


def log_gamma():
    return jnp.log1p(-jnp.exp2(-5.0 - jnp.arange(RET_HEADS, dtype=jnp.float32)))


def retention_block(q, k, v, S):
    q = q.astype(jnp.float32)
    k = k.astype(jnp.float32)
    v = v.astype(jnp.float32)
    S = S.astype(jnp.float32)
    L = q.shape[1]
    lg = log_gamma()
    idx = jnp.arange(L, dtype=jnp.float32)
    d_intra = jnp.exp(lg[:, None, None] * jnp.abs(idx[:, None] - idx[None, :]))
    d_in = jnp.exp(lg[None, :] * (idx[:, None] + 1.0))
    d_out = jnp.exp(lg[:, None] * (L - 1.0 - idx[None, :]))
    scores = jnp.einsum('bihd,bjhd->bhij', q, k) * d_intra[None]
    intra = jnp.einsum('bhij,bjhe->bihe', scores, v)
    inter = jnp.einsum('bihd,bhde->bihe', q, S) * d_in[None, :, :, None]
    S_new = S * jnp.exp(lg * L)[None, :, None, None] + jnp.einsum('bjhd,bjhe,hj->bhde', k, v, d_out)
    return intra + inter, S_new


def retention_prompt(q, k, v):
    B, T = q.shape[:2]
    n_pad = (-T) % CHUNK
    padw = ((0, 0), (n_pad, 0), (0, 0), (0, 0))
    nc = (T + n_pad) // CHUNK

    def blocks(a):
        a = jnp.pad(a.astype(jnp.float32), padw)
        return a.reshape(B, nc, CHUNK, RET_HEADS, HEAD_DIM).transpose(1, 0, 2, 3, 4)

    def body(S, blk):
        qb, kb, vb = blk
        o, S = retention_block(qb, kb, vb, S)
        return S, o

    S0 = jnp.zeros((B, RET_HEADS, HEAD_DIM, HEAD_DIM), jnp.float32)
    S_fin, o = lax.scan(body, S0, (blocks(q), blocks(k), blocks(v)))
    o = o.transpose(1, 0, 2, 3, 4).reshape(B, nc * CHUNK, RET_HEADS, HEAD_DIM)[:, n_pad:]
    return o, S_fin


def mixer_inputs(x, pos, g_pre, w_in):
    B, T = x.shape[:2]
    h = rms_norm(x, g_pre)
    proj = h @ w_in
    q, k, v, gate, ga, gb = jnp.split(
        proj, [RET_WIDTH, 2 * RET_WIDTH, 3 * RET_WIDTH, 4 * RET_WIDTH, 4 * RET_WIDTH + CONV_WIDTH], axis=-1)
    q = rope(q.reshape(B, T, RET_HEADS, HEAD_DIM), pos)
    k = rope(k.reshape(B, T, RET_HEADS, HEAD_DIM), pos) * (HEAD_DIM ** -0.5)
    v = v.reshape(B, T, RET_HEADS, HEAD_DIM)
    u = ga * jax.nn.sigmoid(gb)
    return q, k, v, gate, u


def conv_module(u, buf, dw_w, dw_b, ln_g, ln_b):
    ext = jnp.concatenate([buf.astype(u.dtype), u], axis=1)
    y = lax.conv_general_dilated(ext, dw_w[:, None, :].astype(u.dtype), window_strides=(1,), padding='VALID',
                                 dimension_numbers=('NWC', 'WIO', 'NWC'),
                                 feature_group_count=CONV_WIDTH) + dw_b
    y = jax.nn.silu(layer_norm(y, ln_g, ln_b))
    return y, ext[:, -(CONV_K - 1):]


def mixer_outputs(x, ret, gate, u, conv_buf, gn_g, gn_b, dw_w, dw_b, cln_g, cln_b, w_out, g_post_mix,
                  g_pre_mlp, w_mlp_in, w_mlp_out, g_post_mlp):
    B, T = x.shape[:2]
    mu = jnp.mean(ret, axis=-1, keepdims=True)
    var = jnp.mean(jnp.square(ret - mu), axis=-1, keepdims=True)
    rn = ((ret - mu) * lax.rsqrt(var + EPS)).reshape(B, T, RET_WIDTH)
    rn = (rn * gn_g.astype(jnp.float32) + gn_b.astype(jnp.float32)).astype(x.dtype)
    ret_out = rn * jax.nn.silu(gate)
    conv_out, new_buf = conv_module(u, conv_buf, dw_w, dw_b, cln_g, cln_b)
    mix = jnp.concatenate([ret_out, conv_out.astype(x.dtype)], axis=-1) @ w_out
    x = x + rms_norm(mix, g_post_mix)
    h = rms_norm(x, g_pre_mlp)
    f = jnp.square(jax.nn.relu(h @ w_mlp_in)) @ w_mlp_out
    x = x + rms_norm(f, g_post_mlp)
    return x, new_buf


def setup_inputs(seed: int = 0) -> dict:
    key = jax.random.key(seed)
    ks = jax.random.split(key, 19)
    f32 = jnp.float32

    def nrm(k, shape, scale):
        return jax.random.normal(k, shape, f32) * scale

    def gain(k, shape):
        return 1.0 + 0.05 * jax.random.normal(k, shape, f32)

    return {
        "x_prompt": nrm(ks[0], (BATCH, SEQ, D_MODEL), 1.0),
        "x_sample": nrm(ks[1], (DEC_BATCH, DEC_SEQ, D_MODEL), 1.0),
        "state_ret": nrm(ks[2], (DEPTH, DEC_BATCH, RET_HEADS, HEAD_DIM, HEAD_DIM), 0.5),
        "cache_conv": nrm(ks[3], (DEPTH, DEC_BATCH, CONV_K - 1, CONV_WIDTH), 0.5),
        "meta": nrm(ks[4], (N_META, D_MODEL), 1.0),
        "g_pre_mix": gain(ks[5], (DEPTH, D_MODEL)),
        "w_in": nrm(ks[6], (DEPTH, D_MODEL, N_IN), D_MODEL ** -0.5),
        "gn_g": gain(ks[7], (DEPTH, RET_WIDTH)),
        "gn_b": nrm(ks[8], (DEPTH, RET_WIDTH), 0.02),
        "dw_w": nrm(ks[9], (DEPTH, CONV_K, CONV_WIDTH), CONV_K ** -0.5),
        "dw_b": nrm(ks[10], (DEPTH, CONV_WIDTH), 0.02),
        "cln_g": gain(ks[11], (DEPTH, CONV_WIDTH)),
        "cln_b": nrm(ks[12], (DEPTH, CONV_WIDTH), 0.02),
        "w_out": nrm(ks[13], (DEPTH, RET_WIDTH + CONV_WIDTH, D_MODEL), (RET_WIDTH + CONV_WIDTH) ** -0.5),
        "g_post_mix": gain(ks[14], (DEPTH, D_MODEL)),
        "g_pre_mlp": gain(ks[15], (DEPTH, D_MODEL)),
        "w_mlp_in": nrm(ks[16], (DEPTH, D_MODEL, D_FF), D_MODEL ** -0.5),
        "w_mlp_out": nrm(ks[17], (DEPTH, D_FF, D_MODEL), D_FF ** -0.5),
        "g_post_mlp": gain(ks[18], (DEPTH, D_MODEL)),
    }


def reference(x_prompt, x_sample, state_ret, cache_conv, meta, g_pre_mix, w_in, gn_g, gn_b, dw_w, dw_b,
              cln_g, cln_b, w_out, g_post_mix, g_pre_mlp, w_mlp_in, w_mlp_out, g_post_mlp):
    B = x_prompt.shape[0]
    meta_b = jnp.broadcast_to(meta[None].astype(x_prompt.dtype), (B, N_META, D_MODEL))
    xp = jnp.concatenate([meta_b, x_prompt], axis=1)
    xs = x_sample
    pos_p = jnp.arange(xp.shape[1], dtype=jnp.int32)
    pos_s = N_META + PAST_LEN + jnp.arange(xs.shape[1], dtype=jnp.int32)
    ret_p_list, conv_p_list, ret_s_list, conv_s_list = [], [], [], []
    for l in range(DEPTH):
        wts = (gn_g[l], gn_b[l], dw_w[l], dw_b[l], cln_g[l], cln_b[l], w_out[l], g_post_mix[l],
               g_pre_mlp[l], w_mlp_in[l], w_mlp_out[l], g_post_mlp[l])
        q, k, v, gate, u = mixer_inputs(xp, pos_p, g_pre_mix[l], w_in[l])
        ret, S_p = retention_prompt(q, k, v)
        zero_buf = jnp.zeros((B, CONV_K - 1, CONV_WIDTH), u.dtype)
        xp, buf_p = mixer_outputs(xp, ret, gate, u, zero_buf, *wts)
        q, k, v, gate, u = mixer_inputs(xs, pos_s, g_pre_mix[l], w_in[l])
        ret, S_s = retention_block(q, k, v, state_ret[l])
        xs, buf_s = mixer_outputs(xs, ret, gate, u, cache_conv[l], *wts)
        ret_p_list.append(S_p.astype(x_prompt.dtype))
        conv_p_list.append(buf_p)
        ret_s_list.append(S_s.astype(x_sample.dtype))
        conv_s_list.append(buf_s)
    y_prompt = xp[:, N_META:]
    return (y_prompt, xs, jnp.stack(ret_p_list), jnp.stack(conv_p_list), jnp.stack(ret_s_list), jnp.stack(conv_s_list))
```

```python
import numpy as np
import concourse.bass as bass
import concourse.mybir as mybir
from concourse.bass_utils import run_bass_kernel_spmd

F32 = mybir.dt.float32
BF16 = mybir.dt.bfloat16
AF = mybir.ActivationFunctionType
ALU = mybir.AluOpType

NCORES = 8
D = 1024
NBLK = 17
NTOK = NBLK * 128
EPS = 1e-6
GAMMA = [1.0 - 2.0 ** (-5 - h) for h in range(4)]
SCALE = 128.0 ** -0.5
G64 = [g ** 64 for g in GAMMA]
G16 = [g ** 16 for g in GAMMA]

_CL = [("cos", NBLK * 64), ("sin", NBLK * 64), ("dq", NBLK * 4), ("dkB", NBLK * 4),
       ("dstd", 512), ("db0", 512), ("seqm", 4), ("gpre", 8), ("gprb", 1024), ("gmlb", 1024),
       ("gpm", 1024), ("gpo", 1024), ("gng", 512), ("gnb", 512), ("dww", 124), ("dwb", 4), ("clg", 4), ("clb", 4), ("g64t", 512)]
COFF = {}
_o = 0
for _n, _s in _CL:
    COFF[_n] = (_o, _s)
    _o += _s
CTOT = _o
NPRE = 33
CA_TOT = 2 * NPRE * 64 + NPRE * 4


class Buf:
    __slots__ = ("name", "w", "r", "excl")

    def __init__(self, name, excl=False):
        self.name = name
        self.excl = excl
        self.w = None
        self.r = []


class Q:
    def __init__(self, fw, eng, name):
        self.eng = eng
        self.name = name
        self.sem = fw.nc.alloc_semaphore("q_" + name)
        self.count = 0
        self.seen = {}

    def wait(self, tok):
        sem, val = tok
        key = id(sem)
        if self.seen.get(key, 0) >= val:
            return
        self.eng.wait_ge(sem, val)
        self.seen[key] = val


def _tl(w):
    if w is None:
        return []
    return w if isinstance(w, list) else [w]


class FW:
    def __init__(self, nc, n_dma_sems=6):
        self.nc = nc
        self.pe = Q(self, nc.tensor, "pe")
        self.act = Q(self, nc.scalar, "act")
        self.dve = Q(self, nc.vector, "dve")
        self.pool = Q(self, nc.gpsimd, "pool")
        self.sp = Q(self, nc.sync, "sp")
        self.dma_sems = {}
        for q in (self.sp, self.pool):
            self.dma_sems[q.name] = [[nc.alloc_semaphore(f"d_{q.name}_{i}"), 0] for i in range(n_dma_sems)]
        self.dma_rr = {"sp": 0, "pool": 0}
        self.out_tokens = []

    def all_sems(self):
        sems = [q.sem for q in (self.pe, self.act, self.dve, self.pool, self.sp)]
        for v in self.dma_sems.values():
            sems.extend(e[0] for e in v)
        return sems

    def _deps(self, reads, writes, q):
        toks = []
        for b in reads:
            toks.extend(_tl(b.w))
            if b.excl:
                toks.extend(t for t in b.r if t[0] is not q.sem)
        for b in writes:
            toks.extend(_tl(b.w))
            toks.extend(b.r)
        return toks

    def _record(self, tok, reads, writes):
        for b in reads:
            b.r.append(tok)
        for b in writes:
            b.w = tok
            b.r = []

    def op(self, q, fn, reads=(), writes=()):
        for t in self._deps(reads, writes, q):
            q.wait(t)
        ins = fn()
        ins.then_inc(q.sem, 1)
        q.count += 1
        tok = (q.sem, q.count)
        self._record(tok, reads, writes)
        return tok

    def group(self, q, fns, reads=(), writes=()):
        for t in self._deps(reads, writes, q):
            q.wait(t)
        ins = None
        for fn in fns:
            ins = fn()
        ins.then_inc(q.sem, 1)
        q.count += 1
        tok = (q.sem, q.count)
        self._record(tok, reads, writes)
        return tok

    def dma(self, q, out, in_, reads=(), writes=(), multi=False, is_output=False):
        toks = []
        for b in reads:
            toks.extend(_tl(b.w))
        for b in writes:
            if not multi:
                toks.extend(_tl(b.w))
            toks.extend(b.r)
        for t in toks:
            q.wait(t)
        sems = self.dma_sems[q.name]
        i = self.dma_rr[q.name]
        self.dma_rr[q.name] = (i + 1) % len(sems)
        ent = sems[i]
        if ent[1] > 0:
            q.wait((ent[0], ent[1]))
        q.eng.dma_start(out=out, in_=in_).then_inc(ent[0], 16)
        ent[1] += 16
        tok = (ent[0], ent[1])
        for b in reads:
            b.r.append(tok)
        for b in writes:
            if multi and b.w is not None and not b.r:
                b.w = _tl(b.w) + [tok]
            else:
                b.w = tok
            b.r = []
        if is_output:
            self.out_tokens.append(tok)
        return tok

    def handoff(self, old, new):
        toks = []
        for b in old:
            toks.extend(_tl(b.w))
            toks.extend(b.r)
        for b in new:
            b.w = None
            b.r = list(toks)

    def finish(self):
        for t in self.out_tokens:
            self.sp.wait(t)


class Rot:
    def __init__(self, items):
        self.items = items
        self.i = 0

    def next(self):
        it = self.items[self.i]
        self.i = (self.i + 1) % len(self.items)
        return it


def build_program():
    nc = bass.Bass("TRN2", target_bir_lowering=False)
    fw = FW(nc)
    SP, ACT, DVE, POOL, PE = fw.sp, fw.act, fw.dve, fw.pool, fw.pe
    for sem in fw.all_sems():
        nc.gpsimd.sem_clear(sem)
    nc.all_engine_barrier()
    V, S, G, T = nc.vector, nc.scalar, nc.gpsimd, nc.tensor

    def dram(name, shape, dt=F32, kind="ExternalInput"):
        return nc.dram_tensor(name, shape, dt, kind=kind).ap()

    x_all = dram("x_all", [NTOK, D])
    w_in = dram("w_in", [D, 3072])
    w_out = dram("w_out", [D, D])
    w_mi = dram("w_mi", [D, 4096])
    w_mo = dram("w_mo", [4096, D])
    state_s = dram("state_s", [16 * 128, 128])
    cache_s = dram("cache_s", [120, 512])
    cst_d = dram("cst", [128, CTOT])
    y_all = dram("y_all", [NTOK, D], kind="ExternalOutput")
    o_sfin = dram("o_sfin", [128, 512], kind="ExternalOutput")
    o_convp = dram("o_convp", [30, 512], kind="ExternalOutput")
    o_ss = dram("o_ss", [128, 2048], kind="ExternalOutput")
    o_convs = dram("o_convs", [120, 512], kind="ExternalOutput")
    s_in = dram("s_in", [D, 3072], BF16, kind="Internal")
    s_out = dram("s_out", [D, D], BF16, kind="Internal")
    s_mi = dram("s_mi", [D, 4096], BF16, kind="Internal")
    s_mo = dram("s_mo", [4096, D], BF16, kind="Internal")
    dg_scr = dram("dg_scr", [8, 128, 2048], BF16, kind="Internal")
    b_dgscr = Buf("dg_scr")
    x_pre = dram("x_pre", [NPRE * 128, D])
    cstA_d = dram("cstA", [128, CA_TOT])
    b_s_in, b_s_out, b_s_mi, b_s_mo = Buf("s_in"), Buf("s_out"), Buf("s_mi"), Buf("s_mo")

    def sb(name, shape, dt=F32):
        return nc.alloc_sbuf_tensor(name, shape, dt)

    cst = sb("cst_sb", [128, CTOT]); b_cst = Buf("cst")

    def C(name):
        o, s = COFF[name]
        return cst[:, o:o + s]

    ident_f = sb("ident_f", [128, 128]); ident_b = sb("ident_b", [128, 128], BF16); ones_f = sb("ones_f", [128, 128])
    b_ident = Buf("ident")
    NRING = 3
    ring = [sb(f"ring{i}", [128, 8, 512], BF16) for i in range(NRING)]
    b_ring = [Buf(f"ring{i}") for i in range(NRING)]
    ring_i = [0]
    S32 = sb("S32", [128, 512]); b_S32 = Buf("S32")
    Sch = sb("Sch", [128, 9, 512], BF16); b_Sch = [Buf(f"Sch{i}") for i in range(9)]
    gst = sb("gst", [128, 4, 4, 6]); b_gst = [[Buf(f"gst{j}{h}") for h in range(4)] for j in range(4)]
    gmv = sb("gmv", [128, 4, 16]); b_gmv = [[Buf(f"gmv{j}{h}") for h in range(4)] for j in range(4)]
    b_gr = [Buf(f"gr{j}") for j in range(4)]
    x1all = sb("x1all", [128, 4 * D])
    x1 = [x1all[:, i * D:(i + 1) * D] for i in range(4)]; b_x1 = [Buf(f"x1_{i}") for i in range(4)]
    Ss32 = x1all[:, D:3 * D]
    Ssbf = x1all[:, 3 * D:4 * D].bitcast(BF16)
    B_SS32 = [b_x1[1], b_x1[2]]; B_SSBF = [b_x1[3]]
    hT = sb("hT", [128, 8, 512], BF16); b_hT = [Buf(f"hT{i}") for i in range(4)]
    mixT = sb("mixT", [128, 8, 512], BF16); b_mixT = [Buf(f"mixT{i}") for i in range(4)]
    b_mixTc = Buf("mixTc")
    h2T = hT; b_h2T = b_hT
    ARENA = 48 * 1024
    arena = sb("arena", [128, ARENA // 4])
    ab = arena[:].bitcast(BF16)

    def carve_bf(off_bytes, shape):
        n = int(np.prod(shape))
        v = ab[:, off_bytes // 2: off_bytes // 2 + n]
        return v, off_bytes + 2 * n

    def carve_f(off_bytes, shape):
        n = int(np.prod(shape))
        v = arena[:, off_bytes // 4: off_bytes // 4 + n]
        return v, off_bytes + 4 * n

    o = 0
    q_r, o = carve_bf(o, [4, 512]); k_r, o = carve_bf(o, [4, 512]); k_t, o = carve_bf(o, [4, 512])
    v_b, o = carve_bf(o, [4, 512]); sgate, o = carve_bf(o, [4, 512])
    kT, o = carve_bf(o, [4, 512])
    uT, o = carve_f(o, [4, 544]); yc, o = carve_f(o, [4, 512])
    assert o <= ARENA, o
    o2 = 0
    aT, o2 = carve_bf(o2, [32, 512]); f_sb, o2 = carve_f(o2, [4, 1024])
    assert o2 <= ARENA, o2
    q_r = q_r.rearrange("p (b n) -> p b n", b=4); k_r = k_r.rearrange("p (b n) -> p b n", b=4)
    k_t = k_t.rearrange("p (b n) -> p b n", b=4); v_b = v_b.rearrange("p (b n) -> p b n", b=4)
    sgate = sgate.rearrange("p (b n) -> p b n", b=4)
    kT = kT.rearrange("p (b n) -> p b n", b=4)
    qTA = sb("qTA", [128, 4, 4, 128], BF16); qTB = sb("qTB", [128, 4, 4, 128], BF16)
    uT = uT.rearrange("p (c n) -> p c n", c=4); yc = yc.rearrange("p (c n) -> p c n", c=4)
    aT = aT.rearrange("p (k n) -> p k n", k=32); f_sb = f_sb.rearrange("p (b n) -> p b n", b=4)
    b_qr = [Buf(f"qr{i}") for i in range(4)]; b_kr = [Buf(f"kr{i}") for i in range(4)]
    b_kt = [Buf(f"kt{i}") for i in range(4)]; b_vb = [Buf(f"vb{i}") for i in range(4)]
    b_sg = [Buf(f"sg{i}") for i in range(4)]; b_qT = [Buf(f"qT{i}") for i in range(4)]
    b_kT = [Buf(f"kT{i}") for i in range(4)]
    b_uT = [Buf(f"uT{i}") for i in range(4)]; b_yc = [Buf(f"yc{i}") for i in range(4)]
    b_aT = [Buf(f"aT{i}") for i in range(32)]; b_f = [Buf(f"f{i}") for i in range(4)]
    R1_BUFS = b_qr + b_kr + b_kt + b_vb + b_sg + b_qT + b_kT + b_uT + b_yc
    R2_BUFS = b_aT + b_f
    uxs = sb("uxs", [128, 4, 4, 46]); b_uxs = Buf("uxs")
    uxb = sb("uxb", [128, 4, 4, 46], BF16); b_uxb = Buf("uxb")
    cache_tm = sb("cache_tm", [120, 512]); b_cache_tm = Buf("cache_tm")
    qTm = sb("qTm", [128, 4, 4, 128], BF16); b_qTm = Buf("qTm")
    ktm = sb("ktm", [128, 4, 512], BF16); b_ktm = Buf("ktm")
    xb = Rot([(sb(f"xb{i}", [128, D], BF16), Buf(f"xb{i}")) for i in range(4)])
    junk = sb("junk", [128, D], BF16); b_junk = Buf("junk")
    t32 = Rot([(sb(f"t32_{i}", [128, 512]), Buf(f"t32_{i}")) for i in range(5)])
    tb16 = Rot([(sb(f"tb16_{i}", [128, 512], BF16), Buf(f"tb16_{i}")) for i in range(4)])
    sst = Rot([(sb(f"ss{i}", [128, 16]), Buf(f"ss{i}")) for i in range(12)])
    ps = nc.alloc_psum_tensor("ps", [128, 6, 512], F32)
    b_ps = [Buf(f"ps{i}", True) for i in range(6)]
    pt = [nc.alloc_psum_tensor(f"pt{i}", [128, 1024], BF16) for i in range(2)]
    b_pt = [Buf(f"pt{i}", True) for i in range(2)]
    tbanks = Rot([(pt[i], b_pt[i]) for i in range(2)])
    banksA = Rot([(ps[:, i, :], b_ps[i]) for i in range(4)])
    banksB = Rot([(ps[:, i, :], b_ps[i]) for i in range(6)])

    og, _ = COFF["gpre"]
    b_cstg = Buf("cst_gpre")
    fw.dma(SP, cst[:, og:og + 8], cst_d[:, og:og + 8], writes=[b_cstg])

    def load_cst_bulk():
        fw.dma(SP, cst[:, 0:og], cst_d[:, 0:og], writes=[b_cst], multi=True)
        fw.dma(SP, cst[:, og + 8:CTOT], cst_d[:, og + 8:CTOT], writes=[b_cst], multi=True)
    fw.op(POOL, lambda: G.memset(ident_f[:], 0.0), writes=[b_ident])
    fw.op(POOL, lambda: G.affine_select(out=ident_f[:], in_=ident_f[:], pattern=[[-1, 128]], compare_op=ALU.not_equal,
                                        fill=1.0, base=0, channel_multiplier=1), reads=[b_ident], writes=[b_ident])
    fw.op(DVE, lambda: V.tensor_copy(out=ident_b[:], in_=ident_f[:]), reads=[b_ident], writes=[b_ident])
    fw.op(DVE, lambda: V.memset(ones_f[:], 1.0 / 512.0), reads=[b_ident], writes=[b_ident])

    wv_in = w_in.rearrange("(k p) n -> p k n", p=128)
    fw.dma(POOL, ring[0][:], wv_in[:, :, 512:1024], writes=[b_ring[0]])
    fw.dma(POOL, ring[1][:], wv_in[:, :, 1024:1536], writes=[b_ring[1]])
    def prepass(which):
        if which == 0:
            for k in range(8):
                fw.dma(POOL, s_in[k * 128:(k + 1) * 128, :], w_in[k * 128:(k + 1) * 128, :], writes=[b_s_in], multi=True)
        else:
            for k in range(8):
                fw.dma(POOL, s_out[k * 128:(k + 1) * 128, :], w_out[k * 128:(k + 1) * 128, :], writes=[b_s_out], multi=True)
            for k in range(8):
                fw.dma(POOL, s_mi[k * 128:(k + 1) * 128, :], w_mi[k * 128:(k + 1) * 128, :], writes=[b_s_mi], multi=True)
            for k in range(8):
                fw.dma(POOL, s_mo[k * 512:(k + 1) * 512, :], w_mo[k * 512:(k + 1) * 512, :], writes=[b_s_mo], multi=True)

    def cs_tab(name, blk):
        o, _ = COFF[name]
        return cst[:, o + blk * 64: o + (blk + 1) * 64]

    def d_tab(name, blk):
        o, _ = COFF[name]
        return cst[:, o + blk * 4: o + (blk + 1) * 4]

    def norm_T(x_ap, bx, gname, dst3, bdst):
        ss, bss = sst.next()
        xbt, bxb = xb.next()
        fw.op(ACT, lambda: S.activation(out=junk[:], in_=x_ap, func=AF.Square, accum_out=ss[:, 0:1]),
              reads=[bx], writes=[b_junk, bss])
        fw.op(ACT, lambda: S.activation(out=ss[:, 1:2], in_=ss[:, 0:1], func=AF.Sqrt, scale=1.0 / D, bias=EPS),
              reads=[bss], writes=[bss])
        fw.op(DVE, lambda: V.reciprocal(out=ss[:, 2:3], in_=ss[:, 1:2]), reads=[bss], writes=[bss])
        fw.op(DVE, lambda: V.tensor_scalar(out=xbt[:], in0=x_ap, scalar1=ss[:, 2:3], scalar2=None, op0=ALU.mult),
              reads=[bx, bss], writes=[bxb])
        ptt, bpt = tbanks.next()
        fw.group(PE, [(lambda k=k: T.transpose(out=ptt[:, k * 128:(k + 1) * 128], in_=xbt[:, k * 128:(k + 1) * 128],
                                               identity=ident_b[:])) for k in range(8)],
                 reads=[bxb, b_ident], writes=[bpt])
        gb = C(gname).unsqueeze(2).to_broadcast([128, 8, 128])
        fw.op(DVE, lambda: V.tensor_tensor(out=dst3, in0=ptt[:].rearrange("p (k t) -> p k t", k=8), in1=gb, op=ALU.mult),
              reads=[bpt, b_cst], writes=[bdst])

    def norm_part1(items, gbname):
        sss = []
        for (x_ap, bxs, xbt, bxbs, dst3, bdsts) in items:
            ss, bss = sst.next()
            sss.append((ss, bss))
            fw.op(ACT, lambda x_ap=x_ap, ss=ss: S.activation(out=junk[:], in_=x_ap, func=AF.Square, accum_out=ss[:, 0:1]),
                  reads=bxs, writes=[b_junk, bss])
            fw.op(ACT, lambda ss=ss: S.activation(out=ss[:, 1:2], in_=ss[:, 0:1], func=AF.Sqrt, scale=1.0 / D, bias=EPS),
                  reads=[bss], writes=[bss])
        for (x_ap, bxs, xbt, bxbs, dst3, bdsts), (ss, bss) in zip(items, sss):
            fw.op(DVE, lambda ss=ss: V.reciprocal(out=ss[:, 2:3], in_=ss[:, 1:2]), reads=[bss], writes=[bss])
            fw.op(DVE, lambda x_ap=x_ap, xbt=xbt, ss=ss: V.scalar_tensor_tensor(out=xbt, in0=x_ap, scalar=ss[:, 2:3], in1=C(gbname),
                                                                                op0=ALU.mult, op1=ALU.mult),
                  reads=bxs + [bss, b_cst], writes=bxbs)
        return items

    evac_par = [0]

    def norm_part2(items):
        for (x_ap, bxs, xbt, bxbs, dst3, bdsts) in items:
            ptt, bpt = tbanks.next()
            fw.group(PE, [(lambda k=k, xbt=xbt, ptt=ptt: T.transpose(out=ptt[:, k * 128:(k + 1) * 128],
                                                                     in_=xbt[:, k * 128:(k + 1) * 128], identity=ident_b[:]))
                          for k in range(8)], reads=bxbs + [b_ident], writes=[bpt])
            src = ptt[:].rearrange("p (k t) -> p k t", k=8)
            evac_par[0] ^= 1
            if evac_par[0]:
                fw.op(ACT, lambda dst3=dst3, src=src: S.activation(out=dst3, in_=src, func=AF.Identity), reads=[bpt], writes=bdsts)
            else:
                fw.op(DVE, lambda dst3=dst3, src=src: V.tensor_copy(out=dst3, in_=src), reads=[bpt], writes=bdsts)

    def rope(pb, bpb, blk, out0, out1, bouts, out_fp32, tabs=None):
        tA, btA = t32.next()
        tB, btB = t32.next()
        pq = pb.rearrange("p (h t d) -> p h t d", h=4, t=2)
        a4 = tA[:].rearrange("p (h t d) -> p h t d", h=4, t=2)
        b4 = tB[:].rearrange("p (h t d) -> p h t d", h=4, t=2)
        ctab, stab, btab = tabs if tabs is not None else (cs_tab("cos", blk), cs_tab("sin", blk), b_cst)
        cb = ctab.unsqueeze(1).unsqueeze(1).to_broadcast([128, 4, 2, 64])
        sbb = stab.unsqueeze(1).unsqueeze(1).to_broadcast([128, 4, 2, 64])
        fw.op(DVE, lambda: V.tensor_tensor(out=a4, in0=pq, in1=cb, op=ALU.mult), reads=[bpb, btab], writes=[btA])
        fw.op(DVE, lambda: V.tensor_tensor(out=b4, in0=pq, in1=sbb, op=ALU.mult), reads=[bpb, btab], writes=[btB])
        fw.op(DVE, lambda: V.tensor_tensor(out=out0, in0=a4[:, :, 0, :], in1=b4[:, :, 1, :], op=ALU.subtract), reads=[btA, btB], writes=bouts)
        fw.op(DVE, lambda: V.tensor_tensor(out=out1, in0=a4[:, :, 1, :], in1=b4[:, :, 0, :], op=ALU.add), reads=[btA, btB], writes=bouts)

    def ring_load(src_ap, bsrc, q=None):
        i = ring_i[0]
        ring_i[0] = (i + 1) % NRING
        fw.dma(q or SP, ring[i][:], src_ap, reads=[bsrc], writes=[b_ring[i]])
        return ring[i], b_ring[i]

    def chunk_ap(scr, kg, cg):
        return scr[kg * 1024:(kg + 1) * 1024, cg * 512:(cg + 1) * 512].rearrange("(k p) n -> p k n", p=128)

    W32 = {"in": w_in, "out": w_out, "mi": w_mi, "mo": w_mo}
    WSC = {"in": (s_in, b_s_in), "out": (s_out, b_s_out), "mi": (s_mi, b_s_mi), "mo": (s_mo, b_s_mo)}
    direct = [False]

    def wload(name, kg, cg):
        if direct[0]:
            i = ring_i[0]
            ring_i[0] = (i + 1) % NRING
            fw.dma(POOL, ring[i][:], chunk_ap(W32[name], kg, cg), writes=[b_ring[i]])
            return ring[i], b_ring[i]
        scr, bscr = WSC[name]
        return ring_load(chunk_ap(scr, kg, cg), bscr)

    dww = C("dww").rearrange("p (c k) -> p c k", c=4)

    dgA = qTm[:].rearrange("p a b t -> p (a b) t")
    dgB = ktm[:].rearrange("p a (b t) -> p (a b) t", t=128)

    conv_pending = []

    def diag_build(c, gi):
        dgs, bdgs, k0, k1 = ((dgA, b_qTm, 0, 16), (dgB, b_ktm, 16, 31))[gi]
        n = k1 - k0
        fw.op(DVE, lambda: V.tensor_tensor(out=dgs[:, 0:n, :], in0=ident_b[:].unsqueeze(1).to_broadcast([128, n, 128]),
                                           in1=dww[:, c, k0:k1].unsqueeze(2).to_broadcast([128, n, 128]), op=ALU.mult),
              reads=[b_ident, b_cst], writes=[bdgs])
        fw.dma(SP, dg_scr[2 * c + gi, :, 0:(k1 - k0) * 128].rearrange("p (a t) -> p a t", t=128), dgs[:, 0:k1 - k0, :],
               reads=[bdgs], writes=[b_dgscr], multi=True)

    cstA = arena[:, 0:CA_TOT]; b_cstA = Buf("cstA")
    fw.dma(SP, cstA, cstA_d[:, :], writes=[b_cstA])
    for i in range(2):
        for k in range(8):
            fw.op(DVE, lambda i=i, k=k: V.tensor_scalar(out=ring[i][:, k, :], in0=ring[i][:, k, :], scalar1=C("gpre")[:, k:k + 1],
                                                        scalar2=None, op0=ALU.mult),
                  reads=[b_ring[i], b_cstg], writes=[b_ring[i]])
    accE, b_accE = ps[:, 5, :], b_ps[5]
    A = {}

    def a_stage0(blk):
        xs, bx = x1[blk % 4], b_x1[blk % 4]
        fw.dma(SP, xs, x_pre[blk * 128:(blk + 1) * 128, :], writes=[bx])
        xbt, bxb = xb.next()
        fw.op(ACT, lambda: S.activation(out=xbt[:], in_=xs, func=AF.Copy), reads=[bx], writes=[bxb])
        ss, bss = sst.next()
        A[blk] = dict(xs=xs, bx=bx, ss=ss, bss=bss, xbt=xbt, bxb=bxb)
        fw.op(ACT, lambda: S.activation(out=junk[:], in_=xs, func=AF.Square, accum_out=ss[:, 0:1]), reads=[bx], writes=[b_junk, bss])
        fw.op(ACT, lambda: S.activation(out=ss[:, 1:2], in_=ss[:, 0:1], func=AF.Sqrt, scale=1.0 / D, bias=EPS), reads=[bss], writes=[bss])
        if blk == 2:
            load_cst_bulk()
        if 4 <= blk < 12:
            diag_build((blk - 4) // 2, (blk - 4) % 2)

    def a_stage1(blk):
        a = A[blk]
        ss, bss, xbt, bxb = a["ss"], a["bss"], a["xbt"], a["bxb"]
        fw.op(DVE, lambda: V.reciprocal(out=ss[:, 2:3], in_=ss[:, 1:2]), reads=[bss], writes=[bss])
        ptt, bpt = tbanks.next()
        fw.group(PE, [(lambda k=k: T.transpose(out=ptt[:, k * 128:(k + 1) * 128], in_=xbt[:, k * 128:(k + 1) * 128],
                                               identity=ident_b[:])) for k in range(8)], reads=[bxb, b_ident], writes=[bpt])
        a["ptt"], a["bpt"] = ptt, bpt

    def a_h0(blk):
        dist = (NPRE - 1 - blk) * 128
        return 0 if dist < 1024 else (2 if dist < 2048 else 3)

    def a_stage2a(blk):
        a = A[blk]
        hslot = blk % 4
        c0 = a_h0(blk) * 128
        fw.op(ACT, lambda: S.activation(out=hT[:, :, hslot * 128:(hslot + 1) * 128],
                                        in_=a["ptt"][:].rearrange("p (k t) -> p k t", k=8), func=AF.Identity),
              reads=[a["bpt"]], writes=[b_hT[hslot]])
        pk, bpk = banksA.next()
        fw.group(PE, [(lambda k=k: T.matmul(pk[:, c0:512], lhsT=hT[:, k, hslot * 128:(hslot + 1) * 128], rhs=ring[0][:, k, c0:512],
                                            start=(k == 0), stop=(k == 7))) for k in range(8)],
                 reads=[b_hT[hslot], b_ring[0]], writes=[bpk])
        a.update(pk=pk, bpk=bpk)

    def a_stage2b(blk):
        a = A[blk]
        hslot = blk % 4
        c0 = a_h0(blk) * 128
        pv, bpv = banksA.next()
        fw.group(PE, [(lambda k=k: T.matmul(pv[:, c0:512], lhsT=hT[:, k, hslot * 128:(hslot + 1) * 128], rhs=ring[1][:, k, c0:512],
                                            start=(k == 0), stop=(k == 7))) for k in range(8)],
                 reads=[b_hT[hslot], b_ring[1]], writes=[bpv])
        a.update(pv=pv, bpv=bpv)

    def a_stage3(blk):
        a = A[blk]
        h0 = a_h0(blk)
        nh, c0 = 4 - h0, h0 * 128
        W = nh * 128
        pk, bpk, pv, bpv = a["pk"], a["bpk"], a["pv"], a["bpv"]
        ss, bss = a["ss"], a["bss"]
        tA, btA = t32.next()
        tB, btB = t32.next()
        kr32, bkr32 = t32.next()
        pq = pk[:, c0:512].rearrange("p (h t d) -> p h t d", h=nh, t=2)
        a4 = tA[:, 0:W].rearrange("p (h t d) -> p h t d", h=nh, t=2)
        b4 = tB[:, 0:W].rearrange("p (h t d) -> p h t d", h=nh, t=2)
        k4 = kr32[:, 0:W].rearrange("p (h t d) -> p h t d", h=nh, t=2)
        cb = cstA[:, blk * 64:(blk + 1) * 64].unsqueeze(1).unsqueeze(1).to_broadcast([128, nh, 2, 64])
        sbb = cstA[:, NPRE * 64 + blk * 64: NPRE * 64 + (blk + 1) * 64].unsqueeze(1).unsqueeze(1).to_broadcast([128, nh, 2, 64])
        fw.op(DVE, lambda: V.tensor_tensor(out=a4, in0=pq, in1=cb, op=ALU.mult), reads=[bpk, b_cstA], writes=[btA])
        fw.op(DVE, lambda: V.tensor_tensor(out=b4, in0=pq, in1=sbb, op=ALU.mult), reads=[bpk, b_cstA], writes=[btB])
        fw.op(DVE, lambda: V.tensor_tensor(out=k4[:, :, 0, :], in0=a4[:, :, 0, :], in1=b4[:, :, 1, :], op=ALU.subtract),
              reads=[btA, btB], writes=[bkr32])
        fw.op(DVE, lambda: V.tensor_tensor(out=k4[:, :, 1, :], in0=a4[:, :, 1, :], in1=b4[:, :, 0, :], op=ALU.add),
              reads=[btA, btB, bkr32], writes=[bkr32])
        ktA, bktA = tb16.next()
        vA, bvA = tb16.next()
        dko = 2 * NPRE * 64 + blk * 4
        dk = cstA[:, dko + h0: dko + 4].unsqueeze(2).to_broadcast([128, nh, 128])
        fw.op(DVE, lambda: V.scalar_tensor_tensor(out=ktA[:, c0:512].rearrange("p (h d) -> p h d", h=nh),
                                                  in0=kr32[:, 0:W].rearrange("p (h d) -> p h d", h=nh), scalar=ss[:, 2:3], in1=dk,
                                                  op0=ALU.mult, op1=ALU.mult),
              reads=[bkr32, b_cstA, bss], writes=[bktA])
        fw.op(DVE, lambda: V.tensor_scalar(out=vA[:, c0:512], in0=pv[:, c0:512], scalar1=ss[:, 2:3], scalar2=None, op0=ALU.mult),
              reads=[bpv, bss], writes=[bvA])
        a.update(ktA=ktA, bktA=bktA, vA=vA, bvA=bvA)

    acc_started = [False]

    def a_stage4(blk):
        a = A.pop(blk)
        ktA, vA = a["ktA"], a["vA"]
        fns = []
        for h in range(a_h0(blk), 4):
            first = not acc_started[0]
            acc_started[0] = True
            fns.append(lambda h=h, first=first: T.matmul(accE[:, h * 128:(h + 1) * 128], lhsT=ktA[:, h * 128:(h + 1) * 128],
                                                         rhs=vA[:, h * 128:(h + 1) * 128], start=first,
                                                         stop=(blk == NPRE - 1), skip_group_check=True))
        fw.group(PE, fns, reads=[a["bktA"], a["bvA"]], writes=[b_accE])

    a_order = [(a_stage4, 5), (a_stage3, 4), (a_stage2a, 3), (a_stage1, 2), (a_stage2b, 3), (a_stage0, 0)]
    for t in range(NPRE + 5):
        for fn, off in a_order:
            blk = t - off
            if 0 <= blk < NPRE:
                fn(blk)
    fw.op(DVE, lambda: V.tensor_copy(out=S32[:], in_=accE), reads=[b_accE], writes=[b_S32])

    fw.dma(POOL, Ss32.rearrange("p (s e) -> p s e", s=16), state_s.rearrange("(s d) e -> d s e", d=128), writes=B_SS32)
    fw.op(ACT, lambda: S.activation(out=Ssbf, in_=Ss32, func=AF.Identity), reads=B_SS32, writes=B_SSBF)
    fw.dma(POOL, cache_tm[:], cache_s[:, :], writes=[b_cache_tm])
    co_v = o_convs.rearrange("(s t) c -> s t c", t=30)
    ci_v = cache_s.rearrange("(s t) c -> s t c", t=30)
    for s in range(4):
        fw.dma(POOL, co_v[s, 0:14, :], ci_v[s, 16:30, :], is_output=True)
    fw.op(POOL, lambda: G.memset(qTm[:], 0.0), writes=[b_qTm])
    fw.op(POOL, lambda: G.memset(qTA[:], 0.0), writes=b_qT)
    fw.op(POOL, lambda: G.memset(qTB[:], 0.0), writes=b_qT)
    uhist = sb("uhist", [128, 4, 30]); b_uhist = Buf("uhist")
    ubf = sb("ubf", [128, 4, 544], BF16); b_ubf = [Buf(f"ubf{i}") for i in range(4)]
    diag = Rot([(sb(f"diag{i}", [128, 128], BF16), Buf(f"diag{i}")) for i in range(8)])
    fw.op(POOL, lambda: G.memset(uhist[:], 0.0), writes=[b_uhist])

    def conv_pe(c, N):
        pcv, bpcv = banks.next()
        for gi, (dgs, bdgs, k0, k1) in enumerate(((dgA, b_qTm, 0, 16), (dgB, b_ktm, 16, 31))):
            fw.dma(SP, dgs[:, 0:k1 - k0, :], dg_scr[2 * c + gi, :, 0:(k1 - k0) * 128].rearrange("p (a t) -> p a t", t=128),
                   reads=[b_dgscr], writes=[bdgs])
            if gi == 0:
                conv_flush()
            fw.group(PE, [(lambda k=k: T.matmul(pcv[:, 0:N], lhsT=dgs[:, k - k0, :], rhs=ubf[:, c, k:k + N],
                                                start=(k == 0), stop=(k == 30))) for k in range(k0, k1)],
                     reads=[bdgs, b_ubf[c]], writes=[bpcv])
        conv_pending.append((c, N, pcv, bpcv))

    def conv_flush():
        while conv_pending:
            c, N, pcv, bpcv = conv_pending.pop(0)
            fw.op(ACT, lambda c=c, N=N, pcv=pcv: S.activation(out=yc[:, c, 0:N], in_=pcv[:, 0:N], func=AF.Identity,
                                                               bias=C("dwb")[:, c:c + 1]),
                  reads=[bpcv, b_cst], writes=[b_yc[c]])

    def retention_lockstep(blks):
        nb = len(blks)
        scT4, b_sc = q_r, b_qr
        rob4, b_rob = k_r, b_kr
        ret4, b_ret = uT[:, :, 0:512], b_uT
        fw.op(ACT, lambda: S.activation(out=Sch[:, 0, :], in_=S32[:], func=AF.Identity), reads=[b_S32], writes=[b_Sch[0]])
        for j in range(nb):
            for half in range(2):
                n = 2 * j + half
                rows = slice(half * 64, half * 64 + 64)
                pst, bpst = banks.next()
                fw.group(PE, [(lambda h=h, pst=pst: T.matmul(pst[:, h * 128:(h + 1) * 128], lhsT=k_t[rows, j, h * 128:(h + 1) * 128],
                                                             rhs=v_b[rows, j, h * 128:(h + 1) * 128], start=True, stop=True))
                              for h in range(4)], reads=[b_kt[j], b_vb[j]], writes=[bpst])
                tS, btS = t32.next()
                fw.op(DVE, lambda tS=tS: V.tensor_tensor(out=tS[:], in0=S32[:], in1=C("g64t"), op=ALU.mult),
                      reads=[b_S32, b_cst], writes=[btS])
                fw.op(DVE, lambda tS=tS, pst=pst: V.tensor_tensor(out=S32[:], in0=tS[:], in1=pst, op=ALU.add),
                      reads=[btS, bpst], writes=[b_S32])
                fw.op(ACT, lambda n=n: S.activation(out=Sch[:, n + 1, :], in_=S32[:], func=AF.Identity),
                      reads=[b_S32], writes=[b_Sch[n + 1]])
        for j in range(nb):
            psc, bpsc = banks.next()
            fns = []
            for h in range(4):
                fns.append(lambda h=h, psc=psc: T.matmul(psc[:, h * 128:h * 128 + 64], lhsT=kT[:, j, h * 128:(h + 1) * 128],
                                                         rhs=qTA[:, j, h, 0:64], start=True, stop=True))
                fns.append(lambda h=h, psc=psc: T.matmul(psc[:, h * 128 + 64:(h + 1) * 128], lhsT=kT[:, j, h * 128:(h + 1) * 128],
                                                         rhs=qTB[:, j, h, 64:128], start=True, stop=True))
            fw.group(PE, fns, reads=[b_kT[j], b_qT[j]], writes=[bpsc])
            fw.op(DVE, lambda j=j, psc=psc: V.tensor_tensor(out=scT4[:, j, :], in0=psc, in1=C("dstd"), op=ALU.mult),
                  reads=[bpsc, b_cst], writes=[b_sc[j]])
        def g0(j):
            blk = blks[j]
            pin, bpin = banks.next()
            fw.group(PE, [(lambda h=h, pin=pin: T.matmul(pin[:, h * 128:(h + 1) * 128], lhsT=scT4[:, j, h * 128:(h + 1) * 128],
                                                         rhs=v_b[:, j, h * 128:(h + 1) * 128], start=True, stop=True))
                          for h in range(4)], reads=[b_sc[j], b_vb[j]], writes=[bpin])
            pit, bpit = banks.next()
            fns = []
            for half in range(2):
                qh = qTA if half == 0 else qTB
                for h in range(4):
                    fns.append(lambda h=h, qh=qh, half=half, pit=pit: T.matmul(
                        pit[:, h * 128:(h + 1) * 128], lhsT=qh[:, j, h, :], rhs=Sch[:, 2 * j + half, h * 128:(h + 1) * 128],
                        start=(half == 0 and h == 0), stop=(half == 1), skip_group_check=True))
            fw.group(PE, fns, reads=[b_qT[j], b_Sch[2 * j], b_Sch[2 * j + 1]], writes=[bpit])
            dqb = d_tab("dq", blk).unsqueeze(2).to_broadcast([128, 4, 128])
            fw.op(DVE, lambda: V.tensor_tensor(out=ret4[:, j, :].rearrange("p (h d) -> p h d", h=4),
                                               in0=pit.rearrange("p (h d) -> p h d", h=4), in1=dqb, op=ALU.mult),
                  reads=[bpit, b_cst], writes=[b_ret[j]])
            fw.op(DVE, lambda: V.tensor_tensor(out=ret4[:, j, :], in0=ret4[:, j, :], in1=pin, op=ALU.add),
                  reads=[b_ret[j], bpin], writes=[b_ret[j]])
            conv_pe(j, nb * 128)

        def g1(j):
            for h in range(4):
                fw.op(DVE, lambda h=h: V.bn_stats(out=gst[:, j, h, :], in_=ret4[:, j, h * 128:(h + 1) * 128]),
                      reads=[b_ret[j]], writes=[b_gst[j][h]])
            for h in range(4):
                fw.op(DVE, lambda h=h: V.bn_aggr(out=gmv[:, j, 2 * h:2 * h + 2], in_=gst[:, j, h, :]),
                      reads=[b_gst[j][h]], writes=[b_gmv[j][h]])

        def g2(j):
            mv = gmv[:, j, 0:8].rearrange("p (h t) -> p h t", t=2)
            fw.op(ACT, lambda: S.activation(out=gmv[:, j, 8:12], in_=mv[:, :, 1], func=AF.Sqrt, bias=EPS),
                  reads=b_gmv[j], writes=[b_gr[j]])

        def g3(j):
            fw.op(DVE, lambda: V.reciprocal(out=gmv[:, j, 12:16], in_=gmv[:, j, 8:12]), reads=[b_gr[j]], writes=[b_gr[j]])
            for h in range(4):
                fw.op(DVE, lambda h=h: V.tensor_scalar(out=ret4[:, j, h * 128:(h + 1) * 128], in0=ret4[:, j, h * 128:(h + 1) * 128],
                                                       scalar1=gmv[:, j, 2 * h:2 * h + 1], scalar2=gmv[:, j, 12 + h:13 + h],
                                                       op0=ALU.subtract, op1=ALU.mult),
                      reads=[b_ret[j], b_gr[j]] + b_gmv[j], writes=[b_ret[j]])

        def g4(j):
            fw.op(DVE, lambda: V.tensor_tensor(out=ret4[:, j, :], in0=ret4[:, j, :], in1=C("gng"), op=ALU.mult),
                  reads=[b_ret[j], b_cst], writes=[b_ret[j]])
            fw.op(DVE, lambda: V.tensor_tensor(out=ret4[:, j, :], in0=ret4[:, j, :], in1=C("gnb"), op=ALU.add),
                  reads=[b_ret[j], b_cst], writes=[b_ret[j]])
            fw.op(DVE, lambda: V.tensor_tensor(out=rob4[:, j, :], in0=ret4[:, j, :], in1=sgate[:, j, :], op=ALU.mult),
                  reads=[b_ret[j], b_sg[j]], writes=[b_rob[j]])

        def g5(j):
            ptt, bpt = tbanks.next()
            fw.group(PE, [(lambda h=h, ptt=ptt: T.transpose(out=ptt[:, h * 128:(h + 1) * 128], in_=rob4[:, j, h * 128:(h + 1) * 128],
                                                            identity=ident_b[:])) for h in range(4)],
                     reads=[b_rob[j], b_ident], writes=[bpt])
            fw.op(ACT, lambda ptt=ptt: S.activation(out=mixT[:, 0:4, j * 128:(j + 1) * 128],
                                                    in_=ptt[:, 0:512].rearrange("p (h t) -> p h t", h=4), func=AF.Identity),
                  reads=[bpt], writes=[b_mixT[j]])

        gs = [g0, g1, g2, g3, g4, g5]
        for t in range(nb + len(gs) - 1):
            for si in reversed(range(len(gs))):
                j = t - si
                if 0 <= j < nb:
                    gs[si](j)
        conv_flush()

    ST_BLOCKS = [[0], [1, 2, 3, 4], [5, 6, 7, 8], [9, 10, 11, 12], [13, 14, 15, 16]]
    stg_h = hT[:].rearrange("p k n -> p (k n)").bitcast(F32)
    stg_m = mixT[:].rearrange("p k n -> p (k n)").bitcast(F32)
    STG = [(stg_h[:, 0:D], b_hT), (stg_h[:, D:2 * D], b_hT), (stg_m[:, 0:D], b_mixT + [b_mixTc]), (stg_m[:, D:2 * D], b_mixT + [b_mixTc])]

    def stage0_part1(sti):
        items = []
        for j, blk in enumerate(ST_BLOCKS[sti]):
            stg, bstg = STG[j]
            fw.dma(SP, stg, x_all[blk * 128:(blk + 1) * 128, :], writes=bstg, multi=True)
            xbt, bxb = xb.next()
            items.append((stg, list(bstg), xbt[:], [bxb], hT[:, :, j * 128:(j + 1) * 128], [b_hT[j]]))
        return norm_part1(items, "gprb")

    def stage0_part2(items):
        norm_part2(items)

    banks = banksB
    for sti, blks in enumerate(ST_BLOCKS):
        nb = len(blks)
        N = nb * 128
        is0 = (sti == 0)
        direct[0] = True
        fw.handoff(R2_BUFS + ([b_cstA] if is0 else []), R1_BUFS)
        if is0:
            stage0_part2(stage0_part1(0))
        for j, blk in enumerate(blks):
            fw.dma(SP, x1[j], x_all[blk * 128:(blk + 1) * 128, :], writes=[b_x1[j]])
        for cg in range(4):
            wt, bw = wload("in", 0, cg)
            for j, blk in enumerate(blks):
                pb, bpb = banks.next()
                fw.group(PE, [(lambda k=k: T.matmul(pb, lhsT=hT[:, k, j * 128:(j + 1) * 128], rhs=wt[:, k, :],
                                                    start=(k == 0), stop=(k == 7))) for k in range(8)],
                         reads=[b_hT[j], bw], writes=[bpb])
                if cg == 0:
                    q4 = q_r[:, j, :].rearrange("p (h t d) -> p h t d", h=4, t=2)
                    rope(pb, bpb, blk, q4[:, :, 0, :], q4[:, :, 1, :], [b_qr[j]], False)
                elif cg == 1:
                    kr32, bkr32 = t32.next()
                    k4 = kr32[:].rearrange("p (h t d) -> p h t d", h=4, t=2)
                    rope(pb, bpb, blk, k4[:, :, 0, :], k4[:, :, 1, :], [bkr32], True)
                    fw.op(ACT, lambda j=j, kr32=kr32: S.activation(out=k_r[:, j, :], in_=kr32[:], func=AF.Identity),
                          reads=[bkr32], writes=[b_kr[j]])
                    dk = d_tab("dkB", blk).unsqueeze(2).to_broadcast([128, 4, 128])
                    fw.op(DVE, lambda j=j, kr32=kr32, dk=dk: V.tensor_tensor(
                        out=k_t[:, j, :].rearrange("p (h d) -> p h d", h=4),
                        in0=kr32[:].rearrange("p (h d) -> p h d", h=4), in1=dk, op=ALU.mult),
                          reads=[bkr32, b_cst], writes=[b_kt[j]])
                elif cg == 2:
                    fw.op(ACT, lambda j=j, pb=pb: S.activation(out=v_b[:, j, :], in_=pb, func=AF.Identity),
                          reads=[bpb], writes=[b_vb[j]])
                else:
                    fw.op(ACT, lambda j=j, pb=pb: S.activation(out=sgate[:, j, :], in_=pb, func=AF.Silu),
                          reads=[bpb], writes=[b_sg[j]])
        if not is0:
            for c in range(4):
                fw.op(POOL, lambda c=c: G.tensor_copy(out=ubf[:, c, 0:30], in_=uhist[:, c, :]), reads=[b_uhist], writes=[b_ubf[c]])
        wga, bwga = wload("in", 0, 4)
        for c in range(4):
            pb, bpb = banks.next()
            fw.group(PE, [(lambda k=k: T.matmul(pb[:, 0:N], lhsT=wga[:, k, c * 128:(c + 1) * 128], rhs=hT[:, k, 0:N],
                                                start=(k == 0), stop=(k == 7))) for k in range(8)],
                     reads=b_hT[0:nb] + [bwga], writes=[bpb])
            fw.op(ACT, lambda c=c, pb=pb: S.activation(out=uT[:, c, 30:30 + N], in_=pb[:, 0:N], func=AF.Identity),
                  reads=[bpb], writes=[b_uT[c]])
        wgb, bwgb = wload("in", 0, 5)
        for c in range(4):
            pb, bpb = banks.next()
            fw.group(PE, [(lambda k=k: T.matmul(pb[:, 0:N], lhsT=wgb[:, k, c * 128:(c + 1) * 128], rhs=hT[:, k, 0:N],
                                                start=(k == 0), stop=(k == 7))) for k in range(8)],
                     reads=b_hT[0:nb] + [bwgb], writes=[bpb])
            sg_, bsg_ = t32.next()
            fw.op(ACT, lambda pb=pb, sg_=sg_: S.activation(out=sg_[:, 0:N], in_=pb[:, 0:N], func=AF.Sigmoid),
                  reads=[bpb], writes=[bsg_])
            fw.op(DVE, lambda c=c, sg_=sg_: V.tensor_tensor(out=uT[:, c, 30:30 + N], in0=uT[:, c, 30:30 + N], in1=sg_[:, 0:N],
                                                            op=ALU.mult),
                  reads=[b_uT[c], bsg_], writes=[b_uT[c]])
            if not is0:
                fw.op(ACT, lambda c=c: S.activation(out=ubf[:, c, 30:30 + N], in_=uT[:, c, 30:30 + N], func=AF.Identity),
                      reads=[b_uT[c]], writes=[b_ubf[c]])
        if not is0:
            if sti < len(ST_BLOCKS) - 1:
                for c in range(4):
                    fw.op(POOL, lambda c=c: G.tensor_copy(out=uhist[:, c, :], in_=uT[:, c, N:N + 30]), reads=[b_uT[c]], writes=[b_uhist])
            else:
                pb, bpb = banks.next()
                fw.group(PE, [(lambda c=c: T.transpose(out=pb[0:30, c * 128:(c + 1) * 128], in_=uT[:, c, N:N + 30],
                                                       identity=ident_f[:])) for c in range(4)],
                         reads=b_uT + [b_ident], writes=[bpb])
                utm, butm = t32.next()
                fw.op(DVE, lambda pb=pb, utm=utm: V.tensor_copy(out=utm[0:30, :], in_=pb[0:30, :]), reads=[bpb], writes=[butm])
                fw.dma(POOL, o_convp[:, :], utm[0:30, :], reads=[butm], is_output=True)
        for j, blk in enumerate(blks):
            ptt, bpt = tbanks.next()
            fns = [(lambda h=h: T.transpose(out=ptt[:, h * 128:(h + 1) * 128], in_=q_r[:, j, h * 128:(h + 1) * 128],
                                            identity=ident_b[:])) for h in range(4)]
            fns += [(lambda h=h: T.transpose(out=ptt[:, 512 + h * 128: 512 + (h + 1) * 128],
                                             in_=k_r[:, j, h * 128:(h + 1) * 128], identity=ident_b[:])) for h in range(4)]
            fw.group(PE, fns, reads=[b_qr[j], b_kr[j], b_ident], writes=[bpt])
            p4 = ptt[:, 0:512].rearrange("p (h t) -> p h t", h=4)
            fw.op(ACT, lambda j=j, p4=p4: S.activation(out=qTA[:, j, :, 0:64], in_=p4[:, :, 0:64], func=AF.Identity),
                  reads=[bpt], writes=[b_qT[j]])
            fw.op(ACT, lambda j=j, p4=p4: S.activation(out=qTB[:, j, :, 64:128], in_=p4[:, :, 64:128], func=AF.Identity),
                  reads=[bpt, b_qT[j]], writes=[b_qT[j]])
            fw.op(DVE, lambda j=j, ptt=ptt: V.tensor_copy(out=kT[:, j, :], in_=ptt[:, 512:1024]),
                  reads=[bpt], writes=[b_kT[j]])
        if not is0:
            retention_lockstep(blks)
        for j, blk in (enumerate(blks) if is0 else []):
            dname = "db0" if blk == 0 else "dstd"
            psc, bpsc = banks.next()
            fns = []
            for h in range(4):
                fns.append(lambda h=h: T.matmul(psc[:, h * 128:h * 128 + 64], lhsT=kT[:, j, h * 128:(h + 1) * 128],
                                                rhs=qTA[:, j, h, 0:64], start=True, stop=True))
                fns.append(lambda h=h: T.matmul(psc[:, h * 128 + 64:(h + 1) * 128], lhsT=kT[:, j, h * 128:(h + 1) * 128],
                                                rhs=qTB[:, j, h, 64:128], start=True, stop=True))
            fw.group(PE, fns, reads=[b_kT[j], b_qT[j]], writes=[bpsc])
            scT, bscT = tb16.next()
            fw.op(DVE, lambda psc=psc, scT=scT, dname=dname: V.tensor_tensor(out=scT[:], in0=psc, in1=C(dname), op=ALU.mult),
                  reads=[bpsc, b_cst], writes=[bscT])
            pin, bpin = banks.next()
            fw.group(PE, [(lambda h=h: T.matmul(pin[:, h * 128:(h + 1) * 128], lhsT=scT[:, h * 128:(h + 1) * 128],
                                                rhs=v_b[:, j, h * 128:(h + 1) * 128], start=True, stop=True))
                          for h in range(4)], reads=[bscT, b_vb[j]], writes=[bpin])
            pit, bpit = banks.next()
            if blk == 0:
                for s in range(4):
                    cs = slice(64 + 16 * s, 64 + 16 * s + 16)
                    fw.op(DVE, lambda s=s, cs=cs: V.tensor_copy(out=qTm[:, s, :, cs], in_=qTB[:, 0, :, cs]),
                          reads=[b_qT[0]], writes=[b_qTm])
                for h in range(4):
                    fw.group(PE, [(lambda s=s, h=h: T.matmul(pit[:, h * 128:(h + 1) * 128], lhsT=qTm[:, s, h, :],
                                                             rhs=Ssbf[:, (s * 4 + h) * 128:(s * 4 + h + 1) * 128],
                                                             start=(s == 0), stop=(s == 3), skip_group_check=True))
                                  for s in range(4)], reads=[b_qTm] + B_SSBF, writes=[bpit])
                for s in range(4):
                    fw.op(DVE, lambda s=s: V.tensor_scalar(out=ktm[:, s, :], in0=k_t[:, 0, :],
                                                           scalar1=C("seqm")[:, s:s + 1], scalar2=None, op0=ALU.mult),
                          reads=[b_kt[0], b_cst], writes=[b_ktm])
                for s in range(4):
                    pst, bpst = banks.next()
                    fw.group(PE, [(lambda h=h: T.matmul(pst[:, h * 128:(h + 1) * 128], lhsT=ktm[:, s, h * 128:(h + 1) * 128],
                                                        rhs=v_b[:, 0, h * 128:(h + 1) * 128], start=True, stop=True))
                                  for h in range(4)], reads=[b_ktm, b_vb[0]], writes=[bpst])
                    for h in range(4):
                        sl = slice((s * 4 + h) * 128, (s * 4 + h + 1) * 128)
                        fw.op(DVE, lambda h=h, sl=sl, pst=pst: V.scalar_tensor_tensor(
                            out=Ss32[:, sl], in0=Ss32[:, sl], scalar=float(G16[h]), in1=pst[:, h * 128:(h + 1) * 128],
                            op0=ALU.mult, op1=ALU.add), reads=B_SS32 + [bpst], writes=B_SS32)
                fw.dma(POOL, o_ss[:, :], Ss32, reads=B_SS32, is_output=True)
            ret, bret = t32.next()
            dqb = d_tab("dq", blk).unsqueeze(2).to_broadcast([128, 4, 128])
            fw.op(DVE, lambda ret=ret, pit=pit, dqb=dqb: V.tensor_tensor(out=ret[:].rearrange("p (h d) -> p h d", h=4),
                                                                         in0=pit.rearrange("p (h d) -> p h d", h=4), in1=dqb,
                                                                         op=ALU.mult),
                  reads=[bpit, b_cst], writes=[bret])
            fw.op(DVE, lambda ret=ret, pin=pin: V.tensor_tensor(out=ret[:], in0=ret[:], in1=pin, op=ALU.add),
                  reads=[bret, bpin], writes=[bret])
            ss, bss = sst.next()
            st6, bst6 = sst.next()
            for h in range(4):
                fw.op(DVE, lambda h=h, ret=ret, st6=st6: V.bn_stats(out=st6[:, 0:6], in_=ret[:, h * 128:(h + 1) * 128]),
                      reads=[bret], writes=[bst6])
                fw.op(DVE, lambda h=h, ss=ss, st6=st6: V.bn_aggr(out=ss[:, 2 * h:2 * h + 2], in_=st6[:, 0:6]),
                      reads=[bst6], writes=[bss])
            mv = ss[:, 0:8].rearrange("p (h t) -> p h t", t=2)
            fw.op(ACT, lambda ss=ss, mv=mv: S.activation(out=ss[:, 8:12], in_=mv[:, :, 1], func=AF.Sqrt, bias=EPS),
                  reads=[bss], writes=[bss])
            fw.op(DVE, lambda ss=ss: V.reciprocal(out=ss[:, 12:16], in_=ss[:, 8:12]), reads=[bss], writes=[bss])
            for h in range(4):
                fw.op(DVE, lambda h=h, ret=ret, ss=ss: V.tensor_scalar(out=ret[:, h * 128:(h + 1) * 128],
                                                                       in0=ret[:, h * 128:(h + 1) * 128],
                                                                       scalar1=ss[:, 2 * h:2 * h + 1], scalar2=ss[:, 12 + h:13 + h],
                                                                       op0=ALU.subtract, op1=ALU.mult),
                      reads=[bret, bss], writes=[bret])
            fw.op(DVE, lambda ret=ret: V.tensor_tensor(out=ret[:], in0=ret[:], in1=C("gng"), op=ALU.mult),
                  reads=[bret, b_cst], writes=[bret])
            fw.op(DVE, lambda ret=ret: V.tensor_tensor(out=ret[:], in0=ret[:], in1=C("gnb"), op=ALU.add),
                  reads=[bret, b_cst], writes=[bret])
            rob, brob = tb16.next()
            fw.op(DVE, lambda ret=ret, rob=rob, j=j: V.tensor_tensor(out=rob[:], in0=ret[:], in1=sgate[:, j, :], op=ALU.mult),
                  reads=[bret, b_sg[j]], writes=[brob])
            ptt, bpt = tbanks.next()
            fw.group(PE, [(lambda h=h: T.transpose(out=ptt[:, h * 128:(h + 1) * 128], in_=rob[:, h * 128:(h + 1) * 128],
                                                   identity=ident_b[:])) for h in range(4)],
                     reads=[brob, b_ident], writes=[bpt])
            fw.op(ACT, lambda j=j, ptt=ptt: S.activation(out=mixT[:, 0:4, j * 128:(j + 1) * 128],
                                                         in_=ptt[:, 0:512].rearrange("p (h t) -> p h t", h=4), func=AF.Identity),
                  reads=[bpt], writes=[b_mixT[j]])
        dww = C("dww").rearrange("p (c k) -> p c k", c=4)
        if is0:
            for c in range(4):
                pb, bpb = banks.next()
                fw.op(PE, lambda c=c, pb=pb: T.transpose(out=pb[:, 0:120], in_=cache_tm[:, c * 128:(c + 1) * 128],
                                                         identity=ident_f[0:120, 0:120]),
                      reads=[b_cache_tm, b_ident], writes=[bpb])
                fw.op(DVE, lambda c=c, pb=pb: V.tensor_copy(out=uxs[:, c, :, 0:30],
                                                            in_=pb[:, 0:120].rearrange("p (s t) -> p s t", s=4)),
                      reads=[bpb], writes=[b_uxs])
                fw.op(DVE, lambda c=c: V.tensor_copy(out=uxs[:, c, :, 30:46],
                                                     in_=uT[:, c, 30 + 64:30 + 128].rearrange("p (s t) -> p s t", s=4)),
                      reads=[b_uT[c]], writes=[b_uxs])
            fw.op(ACT, lambda: S.activation(out=uxb[:], in_=uxs[:], func=AF.Identity), reads=[b_uxs], writes=[b_uxb])
            for c in range(4):
                pcv, bpcv = banks.next()
                pcv3 = pcv[:, 0:64].rearrange("p (s t) -> p s t", s=4)
                for gi, (dgs, bdgs, k0, k1) in enumerate(((dgA, b_qTm, 0, 16), (dgB, b_ktm, 16, 31))):
                    fw.dma(SP, dgs[:, 0:k1 - k0, :], dg_scr[2 * c + gi, :, 0:(k1 - k0) * 128].rearrange("p (a t) -> p a t", t=128),
                           reads=[b_dgscr], writes=[bdgs])
                    fw.group(PE, [(lambda k=k, dgs=dgs, k0=k0: T.matmul(pcv3, lhsT=dgs[:, k - k0, :], rhs=uxb[:, c, :, k:k + 16],
                                                                        start=(k == 0), stop=(k == 30))) for k in range(k0, k1)],
                             reads=[bdgs, b_uxb], writes=[bpcv])
                fw.op(DVE, lambda c=c: V.memset(yc[:, c, 0:64], 0.0), writes=[b_yc[c]])
                fw.op(ACT, lambda c=c, pcv=pcv: S.activation(out=yc[:, c, 64:128], in_=pcv[:, 0:64], func=AF.Identity,
                                                             bias=C("dwb")[:, c:c + 1]),
                      reads=[bpcv, b_cst, b_yc[c]], writes=[b_yc[c]])
        if is0:
            pb, bpb = banks.next()
            fw.group(PE, [(lambda c=c: T.transpose(out=pb[0:64, c * 128:(c + 1) * 128], in_=uT[:, c, 30 + 64:30 + 128],
                                                   identity=ident_f[:])) for c in range(4)],
                     reads=b_uT + [b_ident], writes=[bpb])
            utm, butm = t32.next()
            fw.op(DVE, lambda pb=pb, utm=utm: V.tensor_copy(out=utm[0:64, :], in_=pb[0:64, :]), reads=[bpb], writes=[butm])
            for s in range(4):
                fw.dma(POOL, co_v[s, 14:30, :], utm[16 * s:16 * s + 16, :], reads=[butm], is_output=True)
        pmean, bpmean = banks.next()
        pex2, bpex2 = banks.next()
        fw.group(PE, [(lambda c=c: T.matmul(pmean[:, 0:N], lhsT=ones_f[:], rhs=yc[:, c, 0:N], start=(c == 0), stop=(c == 3)))
                      for c in range(4)], reads=b_yc + [b_ident], writes=[bpmean])
        ysq_list = []
        for c in range(4):
            ysq, bysq = t32.next()
            fw.op(ACT, lambda c=c, ysq=ysq: S.activation(out=ysq[:, 0:N], in_=yc[:, c, 0:N], func=AF.Square),
                  reads=[b_yc[c]], writes=[bysq])
            ysq_list.append((ysq, bysq))
        fw.group(PE, [(lambda c=c: T.matmul(pex2[:, 0:N], lhsT=ones_f[:], rhs=ysq_list[c][0][:, 0:N], start=(c == 0), stop=(c == 3)))
                      for c in range(4)], reads=[b for _, b in ysq_list] + [b_ident], writes=[bpex2])
        mean_sb, bmean = t32.next()
        rstd_sb, brstd = t32.next()
        fw.op(ACT, lambda: S.activation(out=mean_sb[:, 0:N], in_=pmean[:, 0:N], func=AF.Identity), reads=[bpmean], writes=[bmean])
        fw.op(DVE, lambda: V.tensor_tensor(out=rstd_sb[:, 0:N], in0=mean_sb[:, 0:N], in1=mean_sb[:, 0:N], op=ALU.mult),
              reads=[bmean], writes=[brstd])
        fw.op(DVE, lambda: V.tensor_tensor(out=rstd_sb[:, 0:N], in0=pex2[:, 0:N], in1=rstd_sb[:, 0:N], op=ALU.subtract),
              reads=[bpex2, brstd], writes=[brstd])
        fw.op(ACT, lambda: S.activation(out=rstd_sb[:, 0:N], in_=rstd_sb[:, 0:N], func=AF.Sqrt, bias=EPS),
              reads=[brstd], writes=[brstd])
        fw.op(DVE, lambda: V.reciprocal(out=rstd_sb[:, 0:N], in_=rstd_sb[:, 0:N]), reads=[brstd], writes=[brstd])
        for c in range(4):
            fw.op(DVE, lambda c=c: V.tensor_tensor(out=yc[:, c, 0:N], in0=yc[:, c, 0:N], in1=mean_sb[:, 0:N], op=ALU.subtract),
                  reads=[b_yc[c], bmean], writes=[b_yc[c]])
            fw.op(DVE, lambda c=c: V.tensor_tensor(out=yc[:, c, 0:N], in0=yc[:, c, 0:N], in1=rstd_sb[:, 0:N], op=ALU.mult),
                  reads=[b_yc[c], brstd], writes=[b_yc[c]])
            fw.op(ACT, lambda c=c: S.activation(out=mixT[:, 4 + c, 0:N], in_=yc[:, c, 0:N], func=AF.Silu,
                                                scale=C("clg")[:, c:c + 1], bias=C("clb")[:, c:c + 1]),
                  reads=[b_yc[c], b_cst], writes=[b_mixTc])
        if is0:
            src_lo = 30 + (64 - 30)
            for c in range(4):
                fw.op(POOL, lambda c=c, src_lo=src_lo: G.tensor_copy(out=uhist[:, c, :], in_=uT[:, c, src_lo:src_lo + 30]),
                      reads=[b_uT[c]], writes=[b_uhist])
        fw.handoff(R1_BUFS, R2_BUFS)
        wo0, bwo0 = wload("out", 0, 0)
        wo1, bwo1 = wload("out", 0, 1)
        def post_norm_residual(gname):
            sss = []
            for j in range(nb):
                ss, bss = sst.next()
                sss.append((ss, bss))
                fw.op(ACT, lambda ss=ss, j=j: S.activation(out=junk[:], in_=f_sb[:, j, :], func=AF.Square, accum_out=ss[:, 0:1]),
                      reads=[b_f[j]], writes=[b_junk, bss])
                fw.op(ACT, lambda ss=ss: S.activation(out=ss[:, 1:2], in_=ss[:, 0:1], func=AF.Sqrt, scale=1.0 / D, bias=EPS),
                      reads=[bss], writes=[bss])
            for j in range(nb):
                ss, bss = sss[j]
                fw.op(DVE, lambda ss=ss: V.reciprocal(out=ss[:, 2:3], in_=ss[:, 1:2]), reads=[bss], writes=[bss])
                fw.op(DVE, lambda ss=ss, j=j: V.scalar_tensor_tensor(out=f_sb[:, j, :], in0=f_sb[:, j, :], scalar=ss[:, 2:3], in1=C(gname),
                                                                     op0=ALU.mult, op1=ALU.mult),
                      reads=[b_f[j], bss, b_cst], writes=[b_f[j]])
                fw.op(DVE, lambda j=j: V.tensor_tensor(out=x1[j], in0=x1[j], in1=f_sb[:, j, :], op=ALU.add),
                      reads=[b_f[j], b_x1[j]], writes=[b_x1[j]])

        P5 = {}

        def s5_0(j):
            for hf, (wo, bwo) in enumerate(((wo0, bwo0), (wo1, bwo1))):
                pm, bpm = banks.next()
                fw.group(PE, [(lambda k=k, pm=pm, wo=wo: T.matmul(pm, lhsT=mixT[:, k, j * 128:(j + 1) * 128], rhs=wo[:, k, :],
                                                                  start=(k == 0), stop=(k == 7))) for k in range(8)],
                         reads=[b_mixT[j], b_mixTc, bwo], writes=[bpm])
                fw.op(ACT, lambda hf=hf, pm=pm: S.activation(out=f_sb[:, j, hf * 512:(hf + 1) * 512], in_=pm, func=AF.Identity),
                      reads=[bpm], writes=[b_f[j]])

        def s5_1(j):
            ss, bss = sst.next()
            P5[j] = dict(ss=ss, bss=bss)
            fw.op(ACT, lambda: S.activation(out=junk[:], in_=f_sb[:, j, :], func=AF.Square, accum_out=ss[:, 0:1]),
                  reads=[b_f[j]], writes=[b_junk, bss])
            fw.op(ACT, lambda: S.activation(out=ss[:, 1:2], in_=ss[:, 0:1], func=AF.Sqrt, scale=1.0 / D, bias=EPS), reads=[bss], writes=[bss])

        def s5_2(j):
            ss, bss = P5[j]["ss"], P5[j]["bss"]
            fw.op(DVE, lambda: V.reciprocal(out=ss[:, 2:3], in_=ss[:, 1:2]), reads=[bss], writes=[bss])
            fw.op(DVE, lambda: V.scalar_tensor_tensor(out=f_sb[:, j, :], in0=f_sb[:, j, :], scalar=ss[:, 2:3], in1=C("gpm"),
                                                      op0=ALU.mult, op1=ALU.mult), reads=[b_f[j], bss, b_cst], writes=[b_f[j]])
            fw.op(DVE, lambda: V.tensor_tensor(out=x1[j], in0=x1[j], in1=f_sb[:, j, :], op=ALU.add),
                  reads=[b_f[j], b_x1[j]], writes=[b_x1[j]])

        def s5_3(j):
            ss, bss = sst.next()
            P5[j].update(ss2=ss, bss2=bss)
            fw.op(ACT, lambda: S.activation(out=junk[:], in_=x1[j], func=AF.Square, accum_out=ss[:, 0:1]),
                  reads=[b_x1[j]], writes=[b_junk, bss])
            fw.op(ACT, lambda: S.activation(out=ss[:, 1:2], in_=ss[:, 0:1], func=AF.Sqrt, scale=1.0 / D, bias=EPS), reads=[bss], writes=[bss])

        def s5_4(j):
            ss, bss = P5[j]["ss2"], P5[j]["bss2"]
            xbt = aT[:, 2 * j:2 * j + 2, :].rearrange("p k n -> p (k n)")
            fw.op(DVE, lambda: V.reciprocal(out=ss[:, 2:3], in_=ss[:, 1:2]), reads=[bss], writes=[bss])
            fw.op(DVE, lambda: V.scalar_tensor_tensor(out=xbt, in0=x1[j], scalar=ss[:, 2:3], in1=C("gmlb"), op0=ALU.mult, op1=ALU.mult),
                  reads=[b_x1[j], bss, b_cst], writes=[b_aT[2 * j], b_aT[2 * j + 1]])

        def s5_5(j):
            xbt = aT[:, 2 * j:2 * j + 2, :].rearrange("p k n -> p (k n)")
            norm_part2([(None, None, xbt, [b_aT[2 * j], b_aT[2 * j + 1]], h2T[:, :, j * 128:(j + 1) * 128], [b_h2T[j]])])

        s5 = [s5_0, s5_1, s5_2, s5_3, s5_4, s5_5]
        for t in range(nb + len(s5) - 1):
            for si in reversed(range(len(s5))):
                j = t - si
                if 0 <= j < nb:
                    s5[si](j)
        for cgi in range(8):
            wt, bw = wload("mi", 0, cgi)
            for c in range(4):
                pb, bpb = banks.next()
                fw.group(PE, [(lambda k=k: T.matmul(pb[:, 0:N], lhsT=wt[:, k, c * 128:(c + 1) * 128], rhs=h2T[:, k, 0:N],
                                                    start=(k == 0), stop=(k == 7))) for k in range(8)],
                         reads=b_h2T[0:nb] + [bw], writes=[bpb])
                rl, brl = t32.next()
                ai = cgi * 4 + c
                fw.op(ACT, lambda pb=pb, rl=rl: S.activation(out=rl[:, 0:N], in_=pb[:, 0:N], func=AF.Relu), reads=[bpb], writes=[brl])
                fw.op(DVE, lambda ai=ai, rl=rl: V.tensor_tensor(out=aT[:, ai, 0:N], in0=rl[:, 0:N], in1=rl[:, 0:N], op=ALU.mult),
                      reads=[brl], writes=[b_aT[ai]])
        nxt = stage0_part1(sti + 1) if sti + 1 < len(ST_BLOCKS) else None
        for hf in range(2):
            accs = [banks.next() for _ in range(nb)]
            for kg in range(4):
                wt, bw = wload("mo", kg, hf)
                for j in range(nb):
                    pacc, bpacc = accs[j]
                    fw.group(PE, [(lambda k=k, pacc=pacc: T.matmul(pacc, lhsT=aT[:, kg * 8 + k, j * 128:(j + 1) * 128], rhs=wt[:, k, :],
                                                                   start=(kg == 0 and k == 0), stop=(kg == 3 and k == 7)))
                                  for k in range(8)],
                             reads=b_aT[kg * 8:(kg + 1) * 8] + [bw], writes=[bpacc])
            for j in range(nb):
                pacc, bpacc = accs[j]
                if j % 2 == 0:
                    fw.op(ACT, lambda j=j, pacc=pacc, hf=hf: S.activation(out=f_sb[:, j, hf * 512:(hf + 1) * 512], in_=pacc,
                                                                          func=AF.Identity), reads=[bpacc], writes=[b_f[j]])
                else:
                    fw.op(DVE, lambda j=j, pacc=pacc, hf=hf: V.tensor_copy(out=f_sb[:, j, hf * 512:(hf + 1) * 512], in_=pacc),
                          reads=[bpacc], writes=[b_f[j]])
        if nxt is not None:
            stage0_part2(nxt)
        post_norm_residual("gpo")
        for j, blk in enumerate(blks):
            fw.dma(SP, y_all[blk * 128:(blk + 1) * 128, :], x1[j], reads=[b_x1[j]], is_output=True)
    fw.dma(POOL, o_sfin[:, :], S32[:], reads=[b_S32], is_output=True)
    fw.finish()
    nc.all_engine_barrier()
    for sem in fw.all_sems():
        nc.gpsimd.sem_clear(sem)
    nc.all_engine_barrier()
    return nc


def _core_consts(c, g_pre_mix, g_pre_mlp, g_post_mix, g_post_mlp, gn_g, gn_b, dw_w, dw_b, cln_g, cln_b):
    f64 = np.float64
    gam = np.array(GAMMA, f64)
    tab = np.zeros((128, CTOT), np.float32)

    def put(name, arr):
        o, s = COFF[name]
        tab[:, o:o + s] = np.asarray(arr, np.float32).reshape(128, s)

    r = np.arange(128)
    pos = np.zeros((128, NBLK), f64)
    il = np.zeros((128, NBLK), f64)
    L = np.full((128, NBLK), 64.0)
    if c == 0:
        pos[:64, 0] = np.maximum(r[:64] - 48, 0)
    else:
        pos[:64, 0] = 16 + 2048 * c - 64 + r[:64]
    pos[64:, 0] = 16 + 4096 + (r[64:] - 64) % 16
    il[:64, 0] = r[:64]
    il[64:, 0] = (r[64:] - 64) % 16
    L[64:, 0] = 16
    for b in range(1, NBLK):
        pos[:, b] = 16 + 2048 * c + (b - 1) * 128 + r
        il[:, b] = r % 64
    inv = 10000.0 ** (-np.arange(64, dtype=f64) / 64.0)
    ang = pos[:, :, None] * inv[None, None, :]
    put("cos", np.cos(ang))
    put("sin", np.sin(ang))
    put("dq", gam[None, None, :] ** (il[:, :, None] + 1.0))
    dkB = SCALE * gam[None, None, :] ** (L[:, :, None] - 1.0 - il[:, :, None])
    put("dkB", dkB)
    jj, ii = np.meshgrid(r, r, indexing="ij")
    same = (jj // 64) == (ii // 64)
    dstd = np.where(same[:, None, :], SCALE * gam[None, :, None] ** np.abs(ii - jj)[:, None, :], 0.0)
    put("dstd", dstd)
    same0 = np.where((jj < 64) | (ii < 64), (jj < 64) & (ii < 64), (jj // 16) == (ii // 16))
    db0 = np.where(same0[:, None, :], SCALE * gam[None, :, None] ** np.abs(ii - jj)[:, None, :], 0.0)
    put("db0", db0)
    seqm = np.zeros((128, 4))
    for s in range(4):
        seqm[64 + 16 * s: 64 + 16 * s + 16, s] = 1.0
    put("seqm", seqm)
    put("gpre", g_pre_mix.reshape(8, 128).T)
    put("gprb", np.broadcast_to(g_pre_mix, (128, 1024)))
    put("gmlb", np.broadcast_to(g_pre_mlp, (128, 1024)))
    put("gpm", np.broadcast_to(g_post_mix, (128, 1024)))
    put("gpo", np.broadcast_to(g_post_mlp, (128, 1024)))
    put("gng", np.broadcast_to(gn_g, (128, 512)))
    put("gnb", np.broadcast_to(gn_b, (128, 512)))
    put("dww", dw_w.reshape(31, 4, 128).transpose(2, 1, 0))
    put("dwb", dw_b.reshape(4, 128).T)
    put("clg", cln_g.reshape(4, 128).T)
    put("clb", cln_b.reshape(4, 128).T)
    put("g64t", np.broadcast_to(np.repeat(np.array(G64), 128), (128, 512)))
    p = 64 + 2048 * c - NPRE * 128 + (np.arange(NPRE)[None, :] * 128 + r[:, None])
    posA = np.maximum(p - 48, 0).astype(f64)
    angA = posA[:, :, None] * inv[None, None, :]
    dist = (64 + 2048 * c - 1 - p).astype(f64)
    dkA = SCALE * gam[None, None, :] ** dist[:, :, None]
    tabA = np.concatenate([np.cos(angA).reshape(128, -1), np.sin(angA).reshape(128, -1), dkA.reshape(128, -1)], 1)
    return tab, np.ascontiguousarray(tabA, dtype=np.float32)


_NC_CACHE = {}


def kernel(x_prompt, x_sample, state_ret, cache_conv, meta, g_pre_mix, w_in, gn_g, gn_b, dw_w, dw_b,
           cln_g, cln_b, w_out, g_post_mix, g_pre_mlp, w_mlp_in, w_mlp_out, g_post_mlp):
    f = lambda a: np.ascontiguousarray(np.asarray(a, dtype=np.float32))
    x_prompt, x_sample, state_ret, cache_conv, meta = map(f, (x_prompt, x_sample, state_ret, cache_conv, meta))
    w_in0, w_out0, w_mi0, w_mo0 = f(w_in)[0], f(w_out)[0], f(w_mlp_in)[0], f(w_mlp_out)[0]
    vecs = [f(v)[0] for v in (g_pre_mix, g_pre_mlp, g_post_mix, g_post_mlp, gn_g, gn_b, dw_w, dw_b, cln_g, cln_b)]
    xp = x_prompt[0]
    if "nc" not in _NC_CACHE:
        _NC_CACHE["nc"] = build_program()
    nc = _NC_CACHE["nc"]
    in_maps = []
    xpad = np.concatenate([np.zeros((48, D), np.float32), meta, xp], 0)
    for c in range(NCORES):
        xa = np.zeros((NTOK, D), np.float32)
        if c == 0:
            xa[48:64] = meta
        else:
            xa[0:64] = xp[2048 * c - 64: 2048 * c]
        xa[64:128] = x_sample[4 * c: 4 * c + 4].reshape(64, D)
        xa[128:] = xp[2048 * c: 2048 * (c + 1)]
        tab, tabA = _core_consts(c, *vecs)
        lo = 64 + 2048 * c - NPRE * 128
        xpre = np.zeros((NPRE * 128, D), np.float32)
        src_lo = max(lo, 0)
        xpre[src_lo - lo:] = xpad[src_lo: 64 + 2048 * c]
        in_maps.append({
            "x_all": xa, "x_pre": xpre, "cstA": tabA, "w_in": w_in0, "w_out": w_out0, "w_mi": w_mi0, "w_mo": w_mo0,
            "state_s": np.ascontiguousarray(state_ret[0, 4 * c: 4 * c + 4].reshape(16 * 128, 128)),
            "cache_s": np.ascontiguousarray(cache_conv[0, 4 * c: 4 * c + 4].reshape(120, 512)),
            "cst": tab,
        })
    res = run_bass_kernel_spmd(nc, in_maps, core_ids=list(range(NCORES)))
    R = res.results
    y_prompt = np.concatenate([R[c]["y_all"][128:] for c in range(NCORES)], 0)[None]
    y_sample = np.concatenate([R[c]["y_all"][64:128].reshape(4, 16, D) for c in range(NCORES)], 0)
    s_p = R[NCORES - 1]["o_sfin"].reshape(128, 4, 128).transpose(1, 0, 2)[None, None]
    conv_p = R[NCORES - 1]["o_convp"][None, None]
    s_s = np.concatenate([R[c]["o_ss"].reshape(128, 4, 4, 128).transpose(1, 2, 0, 3) for c in range(NCORES)], 0)[None]
    conv_s = np.concatenate([R[c]["o_convs"].reshape(4, 30, 512) for c in range(NCORES)], 0)[None]
    return (np.ascontiguousarray(y_prompt, dtype=np.float32), np.ascontiguousarray(y_sample, dtype=np.float32),
            np.ascontiguousarray(s_p, dtype=np.float32), np.ascontiguousarray(conv_p, dtype=np.float32),
            np.ascontiguousarray(s_s, dtype=np.float32), np.ascontiguousarray(conv_s, dtype=np.float32))
```

```python
import numpy as np
import concourse.bass as bass
import concourse.mybir as mybir
from concourse.bass_utils import run_bass_kernel_spmd

F32 = mybir.dt.float32
BF16 = mybir.dt.bfloat16
AF = mybir.ActivationFunctionType
ALU = mybir.AluOpType

NCORES = 8
D = 1024
NBLK = 17
NTOK = NBLK * 128
EPS = 1e-6
GAMMA = [1.0 - 2.0 ** (-5 - h) for h in range(4)]
SCALE = 128.0 ** -0.5
G64 = [g ** 64 for g in GAMMA]
G16 = [g ** 16 for g in GAMMA]

_CL = [("cos", NBLK * 64), ("sin", NBLK * 64), ("dq", NBLK * 4), ("dkB", NBLK * 4),
       ("dstd", 512), ("db0", 512), ("seqm", 4), ("gpre", 8), ("gprb", 1024), ("gmlb", 1024),
       ("gpm", 1024), ("gpo", 1024), ("gng", 512), ("gnb", 512), ("dww", 124), ("dwb", 4), ("clg", 4), ("clb", 4), ("g64t", 512)]
COFF = {}
_o = 0
for _n, _s in _CL:
    COFF[_n] = (_o, _s)
    _o += _s
CTOT = _o
NPRE = 33
CA_TOT = 2 * NPRE * 64 + NPRE * 4


class Buf:
    __slots__ = ("name", "w", "r", "excl")

    def __init__(self, name, excl=False):
        self.name = name
        self.excl = excl
        self.w = None
        self.r = []


class Q:
    def __init__(self, fw, eng, name):
        self.eng = eng
        self.name = name
        self.sem = fw.nc.alloc_semaphore("q_" + name)
        self.count = 0
        self.seen = {}

    def wait(self, tok):
        sem, val = tok
        key = id(sem)
        if self.seen.get(key, 0) >= val:
            return
        self.eng.wait_ge(sem, val)
        self.seen[key] = val


def _tl(w):
    if w is None:
        return []
    return w if isinstance(w, list) else [w]


class FW:
    def __init__(self, nc, n_dma_sems=10):
        self.nc = nc
        self.pe = Q(self, nc.tensor, "pe")
        self.act = Q(self, nc.scalar, "act")
        self.dve = Q(self, nc.vector, "dve")
        self.pool = Q(self, nc.gpsimd, "pool")
        self.sp = Q(self, nc.sync, "sp")
        self.dma_sems = {}
        for q in (self.sp, self.pool):
            self.dma_sems[q.name] = [[nc.alloc_semaphore(f"d_{q.name}_{i}"), 0] for i in range(n_dma_sems)]
        self.dma_rr = {"sp": 0, "pool": 0}
        self.out_tokens = []

    def all_sems(self):
        sems = [q.sem for q in (self.pe, self.act, self.dve, self.pool, self.sp)]
        for v in self.dma_sems.values():
            sems.extend(e[0] for e in v)
        return sems

    def _deps(self, reads, writes, q):
        toks = []
        for b in reads:
            toks.extend(_tl(b.w))
            if b.excl:
                toks.extend(t for t in b.r if t[0] is not q.sem)
        for b in writes:
            toks.extend(_tl(b.w))
            toks.extend(b.r)
        return toks

    def _record(self, tok, reads, writes):
        for b in reads:
            b.r.append(tok)
        for b in writes:
            b.w = tok
            b.r = []

    def op(self, q, fn, reads=(), writes=()):
        for t in self._deps(reads, writes, q):
            q.wait(t)
        ins = fn()
        ins.then_inc(q.sem, 1)
        q.count += 1
        tok = (q.sem, q.count)
        self._record(tok, reads, writes)
        return tok

    def group(self, q, fns, reads=(), writes=()):
        for t in self._deps(reads, writes, q):
            q.wait(t)
        ins = None
        for fn in fns:
            ins = fn()
        ins.then_inc(q.sem, 1)
        q.count += 1
        tok = (q.sem, q.count)
        self._record(tok, reads, writes)
        return tok

    def dma(self, q, out, in_, reads=(), writes=(), multi=False, is_output=False):
        toks = []
        for b in reads:
            toks.extend(_tl(b.w))
        for b in writes:
            if not multi:
                toks.extend(_tl(b.w))
            toks.extend(b.r)
        for t in toks:
            q.wait(t)
        sems = self.dma_sems[q.name]
        i = self.dma_rr[q.name]
        self.dma_rr[q.name] = (i + 1) % len(sems)
        ent = sems[i]
        if ent[1] > 0:
            q.wait((ent[0], ent[1]))
        q.eng.dma_start(out=out, in_=in_).then_inc(ent[0], 16)
        ent[1] += 16
        tok = (ent[0], ent[1])
        for b in reads:
            b.r.append(tok)
        for b in writes:
            if multi and b.w is not None and not b.r:
                b.w = _tl(b.w) + [tok]
            else:
                b.w = tok
            b.r = []
        if is_output:
            self.out_tokens.append(tok)
        return tok

    def handoff(self, old, new):
        toks = []
        for b in old:
            toks.extend(_tl(b.w))
            toks.extend(b.r)
        for b in new:
            b.w = None
            b.r = list(toks)

    def finish(self):
        for t in self.out_tokens:
            self.sp.wait(t)


class Rot:
    def __init__(self, items):
        self.items = items
        self.i = 0

    def next(self):
        it = self.items[self.i]
        self.i = (self.i + 1) % len(self.items)
        return it


def build_program():
    nc = bass.Bass("TRN2", target_bir_lowering=False)
    fw = FW(nc)
    SP, ACT, DVE, POOL, PE = fw.sp, fw.act, fw.dve, fw.pool, fw.pe
    for sem in fw.all_sems():
        nc.gpsimd.sem_clear(sem)
    nc.all_engine_barrier()
    V, S, G, T = nc.vector, nc.scalar, nc.gpsimd, nc.tensor

    def dram(name, shape, dt=F32, kind="ExternalInput"):
        return nc.dram_tensor(name, shape, dt, kind=kind).ap()

    x_all = dram("x_all", [NTOK, D])
    w_in = dram("w_in", [D, 3072])
    w_out = dram("w_out", [D, D])
    w_mi = dram("w_mi", [D, 4096])
    w_mo = dram("w_mo", [4096, D])
    state_s = dram("state_s", [16 * 128, 128])
    cache_s = dram("cache_s", [120, 512])
    cst_d = dram("cst", [128, CTOT])
    y_all = dram("y_all", [NTOK, D], kind="ExternalOutput")
    o_sfin = dram("o_sfin", [128, 512], kind="ExternalOutput")
    o_convp = dram("o_convp", [30, 512], kind="ExternalOutput")
    o_ss = dram("o_ss", [128, 2048], kind="ExternalOutput")
    o_convs = dram("o_convs", [120, 512], kind="ExternalOutput")
    s_in = dram("s_in", [D, 3072], BF16, kind="Internal")
    s_out = dram("s_out", [D, D], BF16, kind="Internal")
    s_mi = dram("s_mi", [D, 4096], BF16, kind="Internal")
    s_mo = dram("s_mo", [4096, D], BF16, kind="Internal")
    dg_scr = dram("dg_scr", [8, 128, 2048], BF16, kind="Internal")
    b_dgscr = Buf("dg_scr")
    x_pre = dram("x_pre", [NPRE * 128, D])
    cstA_d = dram("cstA", [128, CA_TOT])
    b_s_in, b_s_out, b_s_mi, b_s_mo = Buf("s_in"), Buf("s_out"), Buf("s_mi"), Buf("s_mo")

    def sb(name, shape, dt=F32):
        return nc.alloc_sbuf_tensor(name, shape, dt)

    cst = sb("cst_sb", [128, CTOT]); b_cst = Buf("cst")

    def C(name):
        o, s = COFF[name]
        return cst[:, o:o + s]

    ident_f = sb("ident_f", [128, 128]); ident_b = sb("ident_b", [128, 128], BF16); ones_f = sb("ones_f", [128, 128])
    b_ident = Buf("ident")
    NRING = 3
    ring = [sb(f"ring{i}", [128, 8, 512], BF16) for i in range(NRING)]
    b_ring = [Buf(f"ring{i}") for i in range(NRING)]
    ring_i = [0]
    S32 = sb("S32", [128, 512]); b_S32 = Buf("S32")
    Sch = sb("Sch", [128, 9, 512], BF16); b_Sch = [Buf(f"Sch{i}") for i in range(9)]
    gst = sb("gst", [128, 4, 4, 6]); b_gst = [[Buf(f"gst{j}{h}") for h in range(4)] for j in range(4)]
    gmv = sb("gmv", [128, 4, 16]); b_gmv = [[Buf(f"gmv{j}{h}") for h in range(4)] for j in range(4)]
    b_gr = [Buf(f"gr{j}") for j in range(4)]
    x1all = sb("x1all", [128, 4 * D])
    x1 = [x1all[:, i * D:(i + 1) * D] for i in range(4)]; b_x1 = [Buf(f"x1_{i}") for i in range(4)]
    Ss32 = x1all[:, D:3 * D]
    Ssbf = x1all[:, 3 * D:4 * D].bitcast(BF16)
    B_SS32 = [b_x1[1], b_x1[2]]; B_SSBF = [b_x1[3]]
    hT = sb("hT", [128, 8, 512], BF16); b_hT = [Buf(f"hT{i}") for i in range(4)]
    mixT = sb("mixT", [128, 8, 512], BF16); b_mixT = [Buf(f"mixT{i}") for i in range(4)]
    b_mixTc = Buf("mixTc")
    h2T = hT; b_h2T = b_hT
    ARENA = 48 * 1024
    arena = sb("arena", [128, ARENA // 4])
    ab = arena[:].bitcast(BF16)

    def carve_bf(off_bytes, shape):
        n = int(np.prod(shape))
        v = ab[:, off_bytes // 2: off_bytes // 2 + n]
        return v, off_bytes + 2 * n

    def carve_f(off_bytes, shape):
        n = int(np.prod(shape))
        v = arena[:, off_bytes // 4: off_bytes // 4 + n]
        return v, off_bytes + 4 * n

    o = 0
    q_r, o = carve_bf(o, [4, 512]); k_r, o = carve_bf(o, [4, 512]); k_t, o = carve_bf(o, [4, 512])
    v_b, o = carve_bf(o, [4, 512]); sgate, o = carve_bf(o, [4, 512])
    kT, o = carve_bf(o, [4, 512])
    uT, o = carve_f(o, [4, 544]); yc, o = carve_f(o, [4, 512])
    assert o <= ARENA, o
    o2 = 0
    aT, o2 = carve_bf(o2, [32, 512]); f_sb, o2 = carve_f(o2, [4, 1024])
    assert o2 <= ARENA, o2
    q_r = q_r.rearrange("p (b n) -> p b n", b=4); k_r = k_r.rearrange("p (b n) -> p b n", b=4)
    k_t = k_t.rearrange("p (b n) -> p b n", b=4); v_b = v_b.rearrange("p (b n) -> p b n", b=4)
    sgate = sgate.rearrange("p (b n) -> p b n", b=4)
    kT = kT.rearrange("p (b n) -> p b n", b=4)
    qTA = sb("qTA", [128, 4, 4, 128], BF16); qTB = sb("qTB", [128, 4, 4, 128], BF16)
    uT = uT.rearrange("p (c n) -> p c n", c=4); yc = yc.rearrange("p (c n) -> p c n", c=4)
    aT = aT.rearrange("p (k n) -> p k n", k=32); f_sb = f_sb.rearrange("p (b n) -> p b n", b=4)
    b_qr = [Buf(f"qr{i}") for i in range(4)]; b_kr = [Buf(f"kr{i}") for i in range(4)]
    b_kt = [Buf(f"kt{i}") for i in range(4)]; b_vb = [Buf(f"vb{i}") for i in range(4)]
    b_sg = [Buf(f"sg{i}") for i in range(4)]; b_qT = [Buf(f"qT{i}") for i in range(4)]
    b_kT = [Buf(f"kT{i}") for i in range(4)]
    b_uT = [Buf(f"uT{i}") for i in range(4)]; b_yc = [Buf(f"yc{i}") for i in range(4)]
    b_aT = [Buf(f"aT{i}") for i in range(32)]; b_f = [Buf(f"f{i}") for i in range(4)]
    R1_BUFS = b_qr + b_kr + b_kt + b_vb + b_sg + b_qT + b_kT + b_uT + b_yc
    R2_BUFS = b_aT + b_f
    uxs = sb("uxs", [128, 4, 4, 46]); b_uxs = Buf("uxs")
    uxb = sb("uxb", [128, 4, 4, 46], BF16); b_uxb = Buf("uxb")
    cache_tm = sb("cache_tm", [120, 512]); b_cache_tm = Buf("cache_tm")
    qTm = sb("qTm", [128, 4, 4, 128], BF16); b_qTm = Buf("qTm")
    ktm = sb("ktm", [128, 4, 512], BF16); b_ktm = Buf("ktm")
    xb = Rot([(sb(f"xb{i}", [128, D], BF16), Buf(f"xb{i}")) for i in range(4)])
    junk = sb("junk", [128, D], BF16); b_junk = Buf("junk")
    t32 = Rot([(sb(f"t32_{i}", [128, 512]), Buf(f"t32_{i}")) for i in range(5)])
    tb16 = Rot([(sb(f"tb16_{i}", [128, 512], BF16), Buf(f"tb16_{i}")) for i in range(4)])
    sst = Rot([(sb(f"ss{i}", [128, 16]), Buf(f"ss{i}")) for i in range(12)])
    ps = nc.alloc_psum_tensor("ps", [128, 6, 512], F32)
    b_ps = [Buf(f"ps{i}", True) for i in range(6)]
    pt = [nc.alloc_psum_tensor(f"pt{i}", [128, 1024], BF16) for i in range(2)]
    b_pt = [Buf(f"pt{i}", True) for i in range(2)]
    tbanks = Rot([(pt[i], b_pt[i]) for i in range(2)])
    banksA = Rot([(ps[:, i, :], b_ps[i]) for i in range(4)])
    banksB = Rot([(ps[:, i, :], b_ps[i]) for i in range(6)])

    og, _ = COFF["gpre"]
    b_cstg = Buf("cst_gpre")
    fw.dma(SP, cst[:, og:og + 8], cst_d[:, og:og + 8], writes=[b_cstg])

    def load_cst_bulk():
        fw.dma(SP, cst[:, 0:og], cst_d[:, 0:og], writes=[b_cst], multi=True)
        fw.dma(SP, cst[:, og + 8:CTOT], cst_d[:, og + 8:CTOT], writes=[b_cst], multi=True)
    fw.op(POOL, lambda: G.memset(ident_f[:], 0.0), writes=[b_ident])
    fw.op(POOL, lambda: G.affine_select(out=ident_f[:], in_=ident_f[:], pattern=[[-1, 128]], compare_op=ALU.not_equal,
                                        fill=1.0, base=0, channel_multiplier=1), reads=[b_ident], writes=[b_ident])
    fw.op(DVE, lambda: V.tensor_copy(out=ident_b[:], in_=ident_f[:]), reads=[b_ident], writes=[b_ident])
    fw.op(DVE, lambda: V.memset(ones_f[:], 1.0 / 512.0), reads=[b_ident], writes=[b_ident])

    wv_in = w_in.rearrange("(k p) n -> p k n", p=128)
    fw.dma(POOL, ring[0][:], wv_in[:, :, 512:1024], writes=[b_ring[0]])
    fw.dma(POOL, ring[1][:], wv_in[:, :, 1024:1536], writes=[b_ring[1]])
    def prepass(which):
        if which == 0:
            for k in range(8):
                fw.dma(POOL, s_in[k * 128:(k + 1) * 128, :], w_in[k * 128:(k + 1) * 128, :], writes=[b_s_in], multi=True)
        else:
            for k in range(8):
                fw.dma(POOL, s_out[k * 128:(k + 1) * 128, :], w_out[k * 128:(k + 1) * 128, :], writes=[b_s_out], multi=True)
            for k in range(8):
                fw.dma(POOL, s_mi[k * 128:(k + 1) * 128, :], w_mi[k * 128:(k + 1) * 128, :], writes=[b_s_mi], multi=True)
            for k in range(8):
                fw.dma(POOL, s_mo[k * 512:(k + 1) * 512, :], w_mo[k * 512:(k + 1) * 512, :], writes=[b_s_mo], multi=True)

    def cs_tab(name, blk):
        o, _ = COFF[name]
        return cst[:, o + blk * 64: o + (blk + 1) * 64]

    def d_tab(name, blk):
        o, _ = COFF[name]
        return cst[:, o + blk * 4: o + (blk + 1) * 4]

    def norm_T(x_ap, bx, gname, dst3, bdst):
        ss, bss = sst.next()
        xbt, bxb = xb.next()
        fw.op(ACT, lambda: S.activation(out=junk[:], in_=x_ap, func=AF.Square, accum_out=ss[:, 0:1]),
              reads=[bx], writes=[b_junk, bss])
        fw.op(ACT, lambda: S.activation(out=ss[:, 1:2], in_=ss[:, 0:1], func=AF.Sqrt, scale=1.0 / D, bias=EPS),
              reads=[bss], writes=[bss])
        fw.op(DVE, lambda: V.reciprocal(out=ss[:, 2:3], in_=ss[:, 1:2]), reads=[bss], writes=[bss])
        fw.op(DVE, lambda: V.tensor_scalar(out=xbt[:], in0=x_ap, scalar1=ss[:, 2:3], scalar2=None, op0=ALU.mult),
              reads=[bx, bss], writes=[bxb])
        ptt, bpt = tbanks.next()
        fw.group(PE, [(lambda k=k: T.transpose(out=ptt[:, k * 128:(k + 1) * 128], in_=xbt[:, k * 128:(k + 1) * 128],
                                               identity=ident_b[:])) for k in range(8)],
                 reads=[bxb, b_ident], writes=[bpt])
        gb = C(gname).unsqueeze(2).to_broadcast([128, 8, 128])
        fw.op(DVE, lambda: V.tensor_tensor(out=dst3, in0=ptt[:].rearrange("p (k t) -> p k t", k=8), in1=gb, op=ALU.mult),
              reads=[bpt, b_cst], writes=[bdst])

    def norm_part1(items, gbname):
        sss = []
        for (x_ap, bxs, xbt, bxbs, dst3, bdsts) in items:
            ss, bss = sst.next()
            sss.append((ss, bss))
            fw.op(ACT, lambda x_ap=x_ap, ss=ss: S.activation(out=junk[:], in_=x_ap, func=AF.Square, accum_out=ss[:, 0:1]),
                  reads=bxs, writes=[b_junk, bss])
            fw.op(ACT, lambda ss=ss: S.activation(out=ss[:, 1:2], in_=ss[:, 0:1], func=AF.Sqrt, scale=1.0 / D, bias=EPS),
                  reads=[bss], writes=[bss])
        for (x_ap, bxs, xbt, bxbs, dst3, bdsts), (ss, bss) in zip(items, sss):
            fw.op(DVE, lambda ss=ss: V.reciprocal(out=ss[:, 2:3], in_=ss[:, 1:2]), reads=[bss], writes=[bss])
            fw.op(DVE, lambda x_ap=x_ap, xbt=xbt, ss=ss: V.scalar_tensor_tensor(out=xbt, in0=x_ap, scalar=ss[:, 2:3], in1=C(gbname),
                                                                                op0=ALU.mult, op1=ALU.mult),
                  reads=bxs + [bss, b_cst], writes=bxbs)
        return items

    evac_par = [0]

    def norm_part2(items):
        for (x_ap, bxs, xbt, bxbs, dst3, bdsts) in items:
            ptt, bpt = tbanks.next()
            fw.group(PE, [(lambda k=k, xbt=xbt, ptt=ptt: T.transpose(out=ptt[:, k * 128:(k + 1) * 128],
                                                                     in_=xbt[:, k * 128:(k + 1) * 128], identity=ident_b[:]))
                          for k in range(8)], reads=bxbs + [b_ident], writes=[bpt])
            src = ptt[:].rearrange("p (k t) -> p k t", k=8)
            evac_par[0] ^= 1
            if evac_par[0]:
                fw.op(ACT, lambda dst3=dst3, src=src: S.activation(out=dst3, in_=src, func=AF.Identity), reads=[bpt], writes=bdsts)
            else:
                fw.op(DVE, lambda dst3=dst3, src=src: V.tensor_copy(out=dst3, in_=src), reads=[bpt], writes=bdsts)

    def rope(pb, bpb, blk, out0, out1, bouts, out_fp32, tabs=None):
        tA, btA = t32.next()
        tB, btB = t32.next()
        pq = pb.rearrange("p (h t d) -> p h t d", h=4, t=2)
        a4 = tA[:].rearrange("p (h t d) -> p h t d", h=4, t=2)
        b4 = tB[:].rearrange("p (h t d) -> p h t d", h=4, t=2)
        ctab, stab, btab = tabs if tabs is not None else (cs_tab("cos", blk), cs_tab("sin", blk), b_cst)
        cb = ctab.unsqueeze(1).unsqueeze(1).to_broadcast([128, 4, 2, 64])
        sbb = stab.unsqueeze(1).unsqueeze(1).to_broadcast([128, 4, 2, 64])
        fw.op(DVE, lambda: V.tensor_tensor(out=a4, in0=pq, in1=cb, op=ALU.mult), reads=[bpb, btab], writes=[btA])
        fw.op(DVE, lambda: V.tensor_tensor(out=b4, in0=pq, in1=sbb, op=ALU.mult), reads=[bpb, btab], writes=[btB])
        fw.op(DVE, lambda: V.tensor_tensor(out=out0, in0=a4[:, :, 0, :], in1=b4[:, :, 1, :], op=ALU.subtract), reads=[btA, btB], writes=bouts)
        fw.op(DVE, lambda: V.tensor_tensor(out=out1, in0=a4[:, :, 1, :], in1=b4[:, :, 0, :], op=ALU.add), reads=[btA, btB], writes=bouts)

    def ring_load(src_ap, bsrc, q=None):
        i = ring_i[0]
        ring_i[0] = (i + 1) % NRING
        fw.dma(q or SP, ring[i][:], src_ap, reads=[bsrc], writes=[b_ring[i]])
        return ring[i], b_ring[i]

    def chunk_ap(scr, kg, cg):
        return scr[kg * 1024:(kg + 1) * 1024, cg * 512:(cg + 1) * 512].rearrange("(k p) n -> p k n", p=128)

    W32 = {"in": w_in, "out": w_out, "mi": w_mi, "mo": w_mo}
    WSC = {"in": (s_in, b_s_in), "out": (s_out, b_s_out), "mi": (s_mi, b_s_mi), "mo": (s_mo, b_s_mo)}
    direct = [False]

    def wload(name, kg, cg):
        if direct[0]:
            i = ring_i[0]
            ring_i[0] = (i + 1) % NRING
            fw.dma(POOL, ring[i][:], chunk_ap(W32[name], kg, cg), writes=[b_ring[i]])
            return ring[i], b_ring[i]
        scr, bscr = WSC[name]
        return ring_load(chunk_ap(scr, kg, cg), bscr)

    dww = C("dww").rearrange("p (c k) -> p c k", c=4)

    dgA = qTm[:].rearrange("p a b t -> p (a b) t")
    dgB = ktm[:].rearrange("p a (b t) -> p (a b) t", t=128)

    conv_pending = []

    def diag_build(c, gi):
        dgs, bdgs, k0, k1 = ((dgA, b_qTm, 0, 16), (dgB, b_ktm, 16, 31))[gi]
        n = k1 - k0
        fw.op(DVE, lambda: V.tensor_tensor(out=dgs[:, 0:n, :], in0=ident_b[:].unsqueeze(1).to_broadcast([128, n, 128]),
                                           in1=dww[:, c, k0:k1].unsqueeze(2).to_broadcast([128, n, 128]), op=ALU.mult),
              reads=[b_ident, b_cst], writes=[bdgs])
        fw.dma(SP, dg_scr[2 * c + gi, :, 0:(k1 - k0) * 128].rearrange("p (a t) -> p a t", t=128), dgs[:, 0:k1 - k0, :],
               reads=[bdgs], writes=[b_dgscr], multi=True)

    cstA = arena[:, 0:CA_TOT]; b_cstA = Buf("cstA")
    fw.dma(SP, cstA, cstA_d[:, :], writes=[b_cstA])
    for i in range(2):
        for k in range(8):
            fw.op(DVE, lambda i=i, k=k: V.tensor_scalar(out=ring[i][:, k, :], in0=ring[i][:, k, :], scalar1=C("gpre")[:, k:k + 1],
                                                        scalar2=None, op0=ALU.mult),
                  reads=[b_ring[i], b_cstg], writes=[b_ring[i]])
    accE, b_accE = ps[:, 5, :], b_ps[5]
    A = {}

    def a_stage0(blk):
        xs, bx = x1[blk % 4], b_x1[blk % 4]
        fw.dma(SP, xs, x_pre[blk * 128:(blk + 1) * 128, :], writes=[bx])
        xbt, bxb = xb.next()
        fw.op(ACT, lambda: S.activation(out=xbt[:], in_=xs, func=AF.Copy), reads=[bx], writes=[bxb])
        ss, bss = sst.next()
        A[blk] = dict(xs=xs, bx=bx, ss=ss, bss=bss, xbt=xbt, bxb=bxb)
        fw.op(ACT, lambda: S.activation(out=junk[:], in_=xs, func=AF.Square, accum_out=ss[:, 0:1]), reads=[bx], writes=[b_junk, bss])
        fw.op(ACT, lambda: S.activation(out=ss[:, 1:2], in_=ss[:, 0:1], func=AF.Sqrt, scale=1.0 / D, bias=EPS), reads=[bss], writes=[bss])
        if blk == 2:
            load_cst_bulk()
        if 4 <= blk < 12:
            diag_build((blk - 4) // 2, (blk - 4) % 2)

    def a_stage1(blk):
        a = A[blk]
        ss, bss, xbt, bxb = a["ss"], a["bss"], a["xbt"], a["bxb"]
        fw.op(DVE, lambda: V.reciprocal(out=ss[:, 2:3], in_=ss[:, 1:2]), reads=[bss], writes=[bss])
        ptt, bpt = tbanks.next()
        fw.group(PE, [(lambda k=k: T.transpose(out=ptt[:, k * 128:(k + 1) * 128], in_=xbt[:, k * 128:(k + 1) * 128],
                                               identity=ident_b[:])) for k in range(8)], reads=[bxb, b_ident], writes=[bpt])
        a["ptt"], a["bpt"] = ptt, bpt

    def a_h0(blk):
        dist = (NPRE - 1 - blk) * 128
        return 0 if dist < 1024 else (2 if dist < 2048 else 3)

    def a_stage2a(blk):
        a = A[blk]
        hslot = blk % 4
        c0 = a_h0(blk) * 128
        fw.op(ACT, lambda: S.activation(out=hT[:, :, hslot * 128:(hslot + 1) * 128],
                                        in_=a["ptt"][:].rearrange("p (k t) -> p k t", k=8), func=AF.Identity),
              reads=[a["bpt"]], writes=[b_hT[hslot]])
        pk, bpk = banksA.next()
        fw.group(PE, [(lambda k=k: T.matmul(pk[:, c0:512], lhsT=hT[:, k, hslot * 128:(hslot + 1) * 128], rhs=ring[0][:, k, c0:512],
                                            start=(k == 0), stop=(k == 7))) for k in range(8)],
                 reads=[b_hT[hslot], b_ring[0]], writes=[bpk])
        a.update(pk=pk, bpk=bpk)

    def a_stage2b(blk):
        a = A[blk]
        hslot = blk % 4
        c0 = a_h0(blk) * 128
        pv, bpv = banksA.next()
        fw.group(PE, [(lambda k=k: T.matmul(pv[:, c0:512], lhsT=hT[:, k, hslot * 128:(hslot + 1) * 128], rhs=ring[1][:, k, c0:512],
                                            start=(k == 0), stop=(k == 7))) for k in range(8)],
                 reads=[b_hT[hslot], b_ring[1]], writes=[bpv])
        a.update(pv=pv, bpv=bpv)

    def a_stage3(blk):
        a = A[blk]
        h0 = a_h0(blk)
        nh, c0 = 4 - h0, h0 * 128
        W = nh * 128
        pk, bpk, pv, bpv = a["pk"], a["bpk"], a["pv"], a["bpv"]
        ss, bss = a["ss"], a["bss"]
        tA, btA = t32.next()
        tB, btB = t32.next()
        kr32, bkr32 = t32.next()
        pq = pk[:, c0:512].rearrange("p (h t d) -> p h t d", h=nh, t=2)
        a4 = tA[:, 0:W].rearrange("p (h t d) -> p h t d", h=nh, t=2)
        b4 = tB[:, 0:W].rearrange("p (h t d) -> p h t d", h=nh, t=2)
        k4 = kr32[:, 0:W].rearrange("p (h t d) -> p h t d", h=nh, t=2)
        cb = cstA[:, blk * 64:(blk + 1) * 64].unsqueeze(1).unsqueeze(1).to_broadcast([128, nh, 2, 64])
        sbb = cstA[:, NPRE * 64 + blk * 64: NPRE * 64 + (blk + 1) * 64].unsqueeze(1).unsqueeze(1).to_broadcast([128, nh, 2, 64])
        fw.op(DVE, lambda: V.tensor_tensor(out=a4, in0=pq, in1=cb, op=ALU.mult), reads=[bpk, b_cstA], writes=[btA])
        fw.op(DVE, lambda: V.tensor_tensor(out=b4, in0=pq, in1=sbb, op=ALU.mult), reads=[bpk, b_cstA], writes=[btB])
        fw.op(DVE, lambda: V.tensor_tensor(out=k4[:, :, 0, :], in0=a4[:, :, 0, :], in1=b4[:, :, 1, :], op=ALU.subtract),
              reads=[btA, btB], writes=[bkr32])
        fw.op(DVE, lambda: V.tensor_tensor(out=k4[:, :, 1, :], in0=a4[:, :, 1, :], in1=b4[:, :, 0, :], op=ALU.add),
              reads=[btA, btB, bkr32], writes=[bkr32])
        ktA, bktA = tb16.next()
        vA, bvA = tb16.next()
        dko = 2 * NPRE * 64 + blk * 4
        dk = cstA[:, dko + h0: dko + 4].unsqueeze(2).to_broadcast([128, nh, 128])
        fw.op(DVE, lambda: V.scalar_tensor_tensor(out=ktA[:, c0:512].rearrange("p (h d) -> p h d", h=nh),
                                                  in0=kr32[:, 0:W].rearrange("p (h d) -> p h d", h=nh), scalar=ss[:, 2:3], in1=dk,
                                                  op0=ALU.mult, op1=ALU.mult),
              reads=[bkr32, b_cstA, bss], writes=[bktA])
        fw.op(DVE, lambda: V.tensor_scalar(out=vA[:, c0:512], in0=pv[:, c0:512], scalar1=ss[:, 2:3], scalar2=None, op0=ALU.mult),
              reads=[bpv, bss], writes=[bvA])
        a.update(ktA=ktA, bktA=bktA, vA=vA, bvA=bvA)

    acc_started = [False]

    def a_stage4(blk):
        a = A.pop(blk)
        ktA, vA = a["ktA"], a["vA"]
        fns = []
        for h in range(a_h0(blk), 4):
            first = not acc_started[0]
            acc_started[0] = True
            fns.append(lambda h=h, first=first: T.matmul(accE[:, h * 128:(h + 1) * 128], lhsT=ktA[:, h * 128:(h + 1) * 128],
                                                         rhs=vA[:, h * 128:(h + 1) * 128], start=first,
                                                         stop=(blk == NPRE - 1), skip_group_check=True))
        fw.group(PE, fns, reads=[a["bktA"], a["bvA"]], writes=[b_accE])

    a_order = [(a_stage4, 5), (a_stage3, 4), (a_stage2a, 3), (a_stage1, 2), (a_stage2b, 3), (a_stage0, 0)]
    for t in range(NPRE + 5):
        for fn, off in a_order:
            blk = t - off
            if 0 <= blk < NPRE:
                fn(blk)
    fw.op(DVE, lambda: V.tensor_copy(out=S32[:], in_=accE), reads=[b_accE], writes=[b_S32])

    fw.dma(POOL, Ss32.rearrange("p (s e) -> p s e", s=16), state_s.rearrange("(s d) e -> d s e", d=128), writes=B_SS32)
    fw.op(ACT, lambda: S.activation(out=Ssbf, in_=Ss32, func=AF.Identity), reads=B_SS32, writes=B_SSBF)
    fw.dma(POOL, cache_tm[:], cache_s[:, :], writes=[b_cache_tm])
    co_v = o_convs.rearrange("(s t) c -> s t c", t=30)
    ci_v = cache_s.rearrange("(s t) c -> s t c", t=30)
    for s in range(4):
        fw.dma(POOL, co_v[s, 0:14, :], ci_v[s, 16:30, :], is_output=True)
    fw.op(POOL, lambda: G.memset(qTm[:], 0.0), writes=[b_qTm])
    fw.op(POOL, lambda: G.memset(qTA[:], 0.0), writes=b_qT)
    fw.op(POOL, lambda: G.memset(qTB[:], 0.0), writes=b_qT)
    uhist = sb("uhist", [128, 4, 30]); b_uhist = Buf("uhist")
    ubf = sb("ubf", [128, 4, 544], BF16); b_ubf = [Buf(f"ubf{i}") for i in range(4)]
    diag = Rot([(sb(f"diag{i}", [128, 128], BF16), Buf(f"diag{i}")) for i in range(8)])
    fw.op(POOL, lambda: G.memset(uhist[:], 0.0), writes=[b_uhist])

    def conv_pe(c, N):
        pcv, bpcv = banks.next()
        for gi, (dgs, bdgs, k0, k1) in enumerate(((dgA, b_qTm, 0, 16), (dgB, b_ktm, 16, 31))):
            fw.dma(SP, dgs[:, 0:k1 - k0, :], dg_scr[2 * c + gi, :, 0:(k1 - k0) * 128].rearrange("p (a t) -> p a t", t=128),
                   reads=[b_dgscr], writes=[bdgs])
            if gi == 0:
                conv_flush()
            fw.group(PE, [(lambda k=k: T.matmul(pcv[:, 0:N], lhsT=dgs[:, k - k0, :], rhs=ubf[:, c, k:k + N],
                                                start=(k == 0), stop=(k == 30))) for k in range(k0, k1)],
                     reads=[bdgs, b_ubf[c]], writes=[bpcv])
        conv_pending.append((c, N, pcv, bpcv))

    def conv_flush():
        while conv_pending:
            c, N, pcv, bpcv = conv_pending.pop(0)
            fw.op(ACT, lambda c=c, N=N, pcv=pcv: S.activation(out=yc[:, c, 0:N], in_=pcv[:, 0:N], func=AF.Identity,
                                                               bias=C("dwb")[:, c:c + 1]),
                  reads=[bpcv, b_cst], writes=[b_yc[c]])

    def retention_lockstep(blks):
        nb = len(blks)
        scT4, b_sc = q_r, b_qr
        rob4, b_rob = k_r, b_kr
        ret4, b_ret = uT[:, :, 0:512], b_uT
        fw.op(ACT, lambda: S.activation(out=Sch[:, 0, :], in_=S32[:], func=AF.Identity), reads=[b_S32], writes=[b_Sch[0]])
        for j in range(nb):
            for half in range(2):
                n = 2 * j + half
                rows = slice(half * 64, half * 64 + 64)
                pst, bpst = banks.next()
                fw.group(PE, [(lambda h=h, pst=pst: T.matmul(pst[:, h * 128:(h + 1) * 128], lhsT=k_t[rows, j, h * 128:(h + 1) * 128],
                                                             rhs=v_b[rows, j, h * 128:(h + 1) * 128], start=True, stop=True))
                              for h in range(4)], reads=[b_kt[j], b_vb[j]], writes=[bpst])
                tS, btS = t32.next()
                fw.op(DVE, lambda tS=tS: V.tensor_tensor(out=tS[:], in0=S32[:], in1=C("g64t"), op=ALU.mult),
                      reads=[b_S32, b_cst], writes=[btS])
                fw.op(DVE, lambda tS=tS, pst=pst: V.tensor_tensor(out=S32[:], in0=tS[:], in1=pst, op=ALU.add),
                      reads=[btS, bpst], writes=[b_S32])
                fw.op(ACT, lambda n=n: S.activation(out=Sch[:, n + 1, :], in_=S32[:], func=AF.Identity),
                      reads=[b_S32], writes=[b_Sch[n + 1]])
        for j in range(nb):
            psc, bpsc = banks.next()
            fns = []
            for h in range(4):
                fns.append(lambda h=h, psc=psc: T.matmul(psc[:, h * 128:h * 128 + 64], lhsT=kT[:, j, h * 128:(h + 1) * 128],
                                                         rhs=qTA[:, j, h, 0:64], start=True, stop=True))
                fns.append(lambda h=h, psc=psc: T.matmul(psc[:, h * 128 + 64:(h + 1) * 128], lhsT=kT[:, j, h * 128:(h + 1) * 128],
                                                         rhs=qTB[:, j, h, 64:128], start=True, stop=True))
            fw.group(PE, fns, reads=[b_kT[j], b_qT[j]], writes=[bpsc])
            fw.op(DVE, lambda j=j, psc=psc: V.tensor_tensor(out=scT4[:, j, :], in0=psc, in1=C("dstd"), op=ALU.mult),
                  reads=[bpsc, b_cst], writes=[b_sc[j]])
        def g0(j):
            blk = blks[j]
            pin, bpin = banks.next()
            fw.group(PE, [(lambda h=h, pin=pin: T.matmul(pin[:, h * 128:(h + 1) * 128], lhsT=scT4[:, j, h * 128:(h + 1) * 128],
                                                         rhs=v_b[:, j, h * 128:(h + 1) * 128], start=True, stop=True))
                          for h in range(4)], reads=[b_sc[j], b_vb[j]], writes=[bpin])
            pit, bpit = banks.next()
            fns = []
            for half in range(2):
                qh = qTA if half == 0 else qTB
                for h in range(4):
                    fns.append(lambda h=h, qh=qh, half=half, pit=pit: T.matmul(
                        pit[:, h * 128:(h + 1) * 128], lhsT=qh[:, j, h, :], rhs=Sch[:, 2 * j + half, h * 128:(h + 1) * 128],
                        start=(half == 0 and h == 0), stop=(half == 1), skip_group_check=True))
            fw.group(PE, fns, reads=[b_qT[j], b_Sch[2 * j], b_Sch[2 * j + 1]], writes=[bpit])
            dqb = d_tab("dq", blk).unsqueeze(2).to_broadcast([128, 4, 128])
            fw.op(DVE, lambda: V.tensor_tensor(out=ret4[:, j, :].rearrange("p (h d) -> p h d", h=4),
                                               in0=pit.rearrange("p (h d) -> p h d", h=4), in1=dqb, op=ALU.mult),
                  reads=[bpit, b_cst], writes=[b_ret[j]])
            fw.op(DVE, lambda: V.tensor_tensor(out=ret4[:, j, :], in0=ret4[:, j, :], in1=pin, op=ALU.add),
                  reads=[b_ret[j], bpin], writes=[b_ret[j]])
            conv_pe(j, nb * 128)

        def g1(j):
            for h in range(4):
                fw.op(DVE, lambda h=h: V.bn_stats(out=gst[:, j, h, :], in_=ret4[:, j, h * 128:(h + 1) * 128]),
                      reads=[b_ret[j]], writes=[b_gst[j][h]])
            for h in range(4):
                fw.op(DVE, lambda h=h: V.bn_aggr(out=gmv[:, j, 2 * h:2 * h + 2], in_=gst[:, j, h, :]),
                      reads=[b_gst[j][h]], writes=[b_gmv[j][h]])

        def g2(j):
            mv = gmv[:, j, 0:8].rearrange("p (h t) -> p h t", t=2)
            fw.op(ACT, lambda: S.activation(out=gmv[:, j, 8:12], in_=mv[:, :, 1], func=AF.Sqrt, bias=EPS),
                  reads=b_gmv[j], writes=[b_gr[j]])

        def g3(j):
            fw.op(DVE, lambda: V.reciprocal(out=gmv[:, j, 12:16], in_=gmv[:, j, 8:12]), reads=[b_gr[j]], writes=[b_gr[j]])
            for h in range(4):
                fw.op(DVE, lambda h=h: V.tensor_scalar(out=ret4[:, j, h * 128:(h + 1) * 128], in0=ret4[:, j, h * 128:(h + 1) * 128],
                                                       scalar1=gmv[:, j, 2 * h:2 * h + 1], scalar2=gmv[:, j, 12 + h:13 + h],
                                                       op0=ALU.subtract, op1=ALU.mult),
                      reads=[b_ret[j], b_gr[j]] + b_gmv[j], writes=[b_ret[j]])

        def g4(j):
            fw.op(DVE, lambda: V.tensor_tensor(out=ret4[:, j, :], in0=ret4[:, j, :], in1=C("gng"), op=ALU.mult),
                  reads=[b_ret[j], b_cst], writes=[b_ret[j]])
            fw.op(DVE, lambda: V.tensor_tensor(out=ret4[:, j, :], in0=ret4[:, j, :], in1=C("gnb"), op=ALU.add),
                  reads=[b_ret[j], b_cst], writes=[b_ret[j]])
            fw.op(DVE, lambda: V.tensor_tensor(out=rob4[:, j, :], in0=ret4[:, j, :], in1=sgate[:, j, :], op=ALU.mult),
                  reads=[b_ret[j], b_sg[j]], writes=[b_rob[j]])

        def g5(j):
            ptt, bpt = tbanks.next()
            fw.group(PE, [(lambda h=h, ptt=ptt: T.transpose(out=ptt[:, h * 128:(h + 1) * 128], in_=rob4[:, j, h * 128:(h + 1) * 128],
                                                            identity=ident_b[:])) for h in range(4)],
                     reads=[b_rob[j], b_ident], writes=[bpt])
            fw.op(ACT, lambda ptt=ptt: S.activation(out=mixT[:, 0:4, j * 128:(j + 1) * 128],
                                                    in_=ptt[:, 0:512].rearrange("p (h t) -> p h t", h=4), func=AF.Identity),
                  reads=[bpt], writes=[b_mixT[j]])

        gs = [g0, g1, g2, g3, g4, g5]
        for t in range(nb + len(gs) - 1):
            for si in reversed(range(len(gs))):
                j = t - si
                if 0 <= j < nb:
                    gs[si](j)
        conv_flush()

    ST_BLOCKS = [[0], [1, 2, 3, 4], [5, 6, 7, 8], [9, 10, 11, 12], [13, 14, 15, 16]]
    stg_h = hT[:].rearrange("p k n -> p (k n)").bitcast(F32)
    stg_m = mixT[:].rearrange("p k n -> p (k n)").bitcast(F32)
    STG = [(stg_h[:, 0:D], b_hT), (stg_h[:, D:2 * D], b_hT), (stg_m[:, 0:D], b_mixT + [b_mixTc]), (stg_m[:, D:2 * D], b_mixT + [b_mixTc])]

    def stage0_part1(sti):
        items = []
        for j, blk in enumerate(ST_BLOCKS[sti]):
            stg, bstg = STG[j]
            fw.dma(SP, stg, x_all[blk * 128:(blk + 1) * 128, :], writes=bstg, multi=True)
            xbt, bxb = xb.next()
            items.append((stg, list(bstg), xbt[:], [bxb], hT[:, :, j * 128:(j + 1) * 128], [b_hT[j]]))
        return norm_part1(items, "gprb")

    def stage0_part2(items):
        norm_part2(items)

    banks = banksB
    for sti, blks in enumerate(ST_BLOCKS):
        nb = len(blks)
        N = nb * 128
        is0 = (sti == 0)
        direct[0] = True
        fw.handoff(R2_BUFS + ([b_cstA] if is0 else []), R1_BUFS)
        if is0:
            stage0_part2(stage0_part1(0))
        for j, blk in enumerate(blks):
            fw.dma(SP, x1[j], x_all[blk * 128:(blk + 1) * 128, :], writes=[b_x1[j]])
        for cg in range(4):
            wt, bw = wload("in", 0, cg)
            for j, blk in enumerate(blks):
                pb, bpb = banks.next()
                fw.group(PE, [(lambda k=k: T.matmul(pb, lhsT=hT[:, k, j * 128:(j + 1) * 128], rhs=wt[:, k, :],
                                                    start=(k == 0), stop=(k == 7))) for k in range(8)],
                         reads=[b_hT[j], bw], writes=[bpb])
                if cg == 0:
                    q4 = q_r[:, j, :].rearrange("p (h t d) -> p h t d", h=4, t=2)
                    rope(pb, bpb, blk, q4[:, :, 0, :], q4[:, :, 1, :], [b_qr[j]], False)
                elif cg == 1:
                    kr32, bkr32 = t32.next()
                    k4 = kr32[:].rearrange("p (h t d) -> p h t d", h=4, t=2)
                    rope(pb, bpb, blk, k4[:, :, 0, :], k4[:, :, 1, :], [bkr32], True)
                    fw.op(ACT, lambda j=j, kr32=kr32: S.activation(out=k_r[:, j, :], in_=kr32[:], func=AF.Identity),
                          reads=[bkr32], writes=[b_kr[j]])
                    dk = d_tab("dkB", blk).unsqueeze(2).to_broadcast([128, 4, 128])
                    fw.op(DVE, lambda j=j, kr32=kr32, dk=dk: V.tensor_tensor(
                        out=k_t[:, j, :].rearrange("p (h d) -> p h d", h=4),
                        in0=kr32[:].rearrange("p (h d) -> p h d", h=4), in1=dk, op=ALU.mult),
                          reads=[bkr32, b_cst], writes=[b_kt[j]])
                elif cg == 2:
                    fw.op(ACT, lambda j=j, pb=pb: S.activation(out=v_b[:, j, :], in_=pb, func=AF.Identity),
                          reads=[bpb], writes=[b_vb[j]])
                else:
                    fw.op(ACT, lambda j=j, pb=pb: S.activation(out=sgate[:, j, :], in_=pb, func=AF.Silu),
                          reads=[bpb], writes=[b_sg[j]])
        if not is0:
            for c in range(4):
                fw.op(POOL, lambda c=c: G.tensor_copy(out=ubf[:, c, 0:30], in_=uhist[:, c, :]), reads=[b_uhist], writes=[b_ubf[c]])
        wga, bwga = wload("in", 0, 4)
        for c in range(4):
            pb, bpb = banks.next()
            fw.group(PE, [(lambda k=k: T.matmul(pb[:, 0:N], lhsT=wga[:, k, c * 128:(c + 1) * 128], rhs=hT[:, k, 0:N],
                                                start=(k == 0), stop=(k == 7))) for k in range(8)],
                     reads=b_hT[0:nb] + [bwga], writes=[bpb])
            fw.op(ACT, lambda c=c, pb=pb: S.activation(out=uT[:, c, 30:30 + N], in_=pb[:, 0:N], func=AF.Identity),
                  reads=[bpb], writes=[b_uT[c]])
        wgb, bwgb = wload("in", 0, 5)
        for c in range(4):
            pb, bpb = banks.next()
            fw.group(PE, [(lambda k=k: T.matmul(pb[:, 0:N], lhsT=wgb[:, k, c * 128:(c + 1) * 128], rhs=hT[:, k, 0:N],
                                                start=(k == 0), stop=(k == 7))) for k in range(8)],
                     reads=b_hT[0:nb] + [bwgb], writes=[bpb])
            sg_, bsg_ = t32.next()
            fw.op(ACT, lambda pb=pb, sg_=sg_: S.activation(out=sg_[:, 0:N], in_=pb[:, 0:N], func=AF.Sigmoid),
                  reads=[bpb], writes=[bsg_])
            fw.op(DVE, lambda c=c, sg_=sg_: V.tensor_tensor(out=uT[:, c, 30:30 + N], in0=uT[:, c, 30:30 + N], in1=sg_[:, 0:N],
                                                            op=ALU.mult),
                  reads=[b_uT[c], bsg_], writes=[b_uT[c]])
            if not is0:
                fw.op(ACT, lambda c=c: S.activation(out=ubf[:, c, 30:30 + N], in_=uT[:, c, 30:30 + N], func=AF.Identity),
                      reads=[b_uT[c]], writes=[b_ubf[c]])
        if not is0:
            if sti < len(ST_BLOCKS) - 1:
                for c in range(4):
                    fw.op(POOL, lambda c=c: G.tensor_copy(out=uhist[:, c, :], in_=uT[:, c, N:N + 30]), reads=[b_uT[c]], writes=[b_uhist])
            else:
                pb, bpb = banks.next()
                fw.group(PE, [(lambda c=c: T.transpose(out=pb[0:30, c * 128:(c + 1) * 128], in_=uT[:, c, N:N + 30],
                                                       identity=ident_f[:])) for c in range(4)],
                         reads=b_uT + [b_ident], writes=[bpb])
                utm, butm = t32.next()
                fw.op(DVE, lambda pb=pb, utm=utm: V.tensor_copy(out=utm[0:30, :], in_=pb[0:30, :]), reads=[bpb], writes=[butm])
                fw.dma(POOL, o_convp[:, :], utm[0:30, :], reads=[butm], is_output=True)
        for j, blk in enumerate(blks):
            ptt, bpt = tbanks.next()
            fns = [(lambda h=h: T.transpose(out=ptt[:, h * 128:(h + 1) * 128], in_=q_r[:, j, h * 128:(h + 1) * 128],
                                            identity=ident_b[:])) for h in range(4)]
            fns += [(lambda h=h: T.transpose(out=ptt[:, 512 + h * 128: 512 + (h + 1) * 128],
                                             in_=k_r[:, j, h * 128:(h + 1) * 128], identity=ident_b[:])) for h in range(4)]
            fw.group(PE, fns, reads=[b_qr[j], b_kr[j], b_ident], writes=[bpt])
            p4 = ptt[:, 0:512].rearrange("p (h t) -> p h t", h=4)
            fw.op(ACT, lambda j=j, p4=p4: S.activation(out=qTA[:, j, :, 0:64], in_=p4[:, :, 0:64], func=AF.Identity),
                  reads=[bpt], writes=[b_qT[j]])
            fw.op(ACT, lambda j=j, p4=p4: S.activation(out=qTB[:, j, :, 64:128], in_=p4[:, :, 64:128], func=AF.Identity),
                  reads=[bpt, b_qT[j]], writes=[b_qT[j]])
            fw.op(DVE, lambda j=j, ptt=ptt: V.tensor_copy(out=kT[:, j, :], in_=ptt[:, 512:1024]),
                  reads=[bpt], writes=[b_kT[j]])
        if not is0:
            retention_lockstep(blks)
        for j, blk in (enumerate(blks) if is0 else []):
            dname = "db0" if blk == 0 else "dstd"
            psc, bpsc = banks.next()
            fns = []
            for h in range(4):
                fns.append(lambda h=h: T.matmul(psc[:, h * 128:h * 128 + 64], lhsT=kT[:, j, h * 128:(h + 1) * 128],
                                                rhs=qTA[:, j, h, 0:64], start=True, stop=True))
                fns.append(lambda h=h: T.matmul(psc[:, h * 128 + 64:(h + 1) * 128], lhsT=kT[:, j, h * 128:(h + 1) * 128],
                                                rhs=qTB[:, j, h, 64:128], start=True, stop=True))
            fw.group(PE, fns, reads=[b_kT[j], b_qT[j]], writes=[bpsc])
            scT, bscT = tb16.next()
            fw.op(DVE, lambda psc=psc, scT=scT, dname=dname: V.tensor_tensor(out=scT[:], in0=psc, in1=C(dname), op=ALU.mult),
                  reads=[bpsc, b_cst], writes=[bscT])
            pin, bpin = banks.next()
            fw.group(PE, [(lambda h=h: T.matmul(pin[:, h * 128:(h + 1) * 128], lhsT=scT[:, h * 128:(h + 1) * 128],
                                                rhs=v_b[:, j, h * 128:(h + 1) * 128], start=True, stop=True))
                          for h in range(4)], reads=[bscT, b_vb[j]], writes=[bpin])
            pit, bpit = banks.next()
            if blk == 0:
                for s in range(4):
                    cs = slice(64 + 16 * s, 64 + 16 * s + 16)
                    fw.op(DVE, lambda s=s, cs=cs: V.tensor_copy(out=qTm[:, s, :, cs], in_=qTB[:, 0, :, cs]),
                          reads=[b_qT[0]], writes=[b_qTm])
                for h in range(4):
                    fw.group(PE, [(lambda s=s, h=h: T.matmul(pit[:, h * 128:(h + 1) * 128], lhsT=qTm[:, s, h, :],
                                                             rhs=Ssbf[:, (s * 4 + h) * 128:(s * 4 + h + 1) * 128],
                                                             start=(s == 0), stop=(s == 3), skip_group_check=True))
                                  for s in range(4)], reads=[b_qTm] + B_SSBF, writes=[bpit])
                for s in range(4):
                    fw.op(DVE, lambda s=s: V.tensor_scalar(out=ktm[:, s, :], in0=k_t[:, 0, :],
                                                           scalar1=C("seqm")[:, s:s + 1], scalar2=None, op0=ALU.mult),
                          reads=[b_kt[0], b_cst], writes=[b_ktm])
                for s in range(4):
                    pst, bpst = banks.next()
                    fw.group(PE, [(lambda h=h: T.matmul(pst[:, h * 128:(h + 1) * 128], lhsT=ktm[:, s, h * 128:(h + 1) * 128],
                                                        rhs=v_b[:, 0, h * 128:(h + 1) * 128], start=True, stop=True))
                                  for h in range(4)], reads=[b_ktm, b_vb[0]], writes=[bpst])
                    for h in range(4):
                        sl = slice((s * 4 + h) * 128, (s * 4 + h + 1) * 128)
                        fw.op(DVE, lambda h=h, sl=sl, pst=pst: V.scalar_tensor_tensor(
                            out=Ss32[:, sl], in0=Ss32[:, sl], scalar=float(G16[h]), in1=pst[:, h * 128:(h + 1) * 128],
                            op0=ALU.mult, op1=ALU.add), reads=B_SS32 + [bpst], writes=B_SS32)
                fw.dma(POOL, o_ss[:, :], Ss32, reads=B_SS32, is_output=True)
            ret, bret = t32.next()
            dqb = d_tab("dq", blk).unsqueeze(2).to_broadcast([128, 4, 128])
            fw.op(DVE, lambda ret=ret, pit=pit, dqb=dqb: V.tensor_tensor(out=ret[:].rearrange("p (h d) -> p h d", h=4),
                                                                         in0=pit.rearrange("p (h d) -> p h d", h=4), in1=dqb,
                                                                         op=ALU.mult),
                  reads=[bpit, b_cst], writes=[bret])
            fw.op(DVE, lambda ret=ret, pin=pin: V.tensor_tensor(out=ret[:], in0=ret[:], in1=pin, op=ALU.add),
                  reads=[bret, bpin], writes=[bret])
            ss, bss = sst.next()
            st6, bst6 = sst.next()
            for h in range(4):
                fw.op(DVE, lambda h=h, ret=ret, st6=st6: V.bn_stats(out=st6[:, 0:6], in_=ret[:, h * 128:(h + 1) * 128]),
                      reads=[bret], writes=[bst6])
                fw.op(DVE, lambda h=h, ss=ss, st6=st6: V.bn_aggr(out=ss[:, 2 * h:2 * h + 2], in_=st6[:, 0:6]),
                      reads=[bst6], writes=[bss])
            mv = ss[:, 0:8].rearrange("p (h t) -> p h t", t=2)
            fw.op(ACT, lambda ss=ss, mv=mv: S.activation(out=ss[:, 8:12], in_=mv[:, :, 1], func=AF.Sqrt, bias=EPS),
                  reads=[bss], writes=[bss])
            fw.op(DVE, lambda ss=ss: V.reciprocal(out=ss[:, 12:16], in_=ss[:, 8:12]), reads=[bss], writes=[bss])
            for h in range(4):
                fw.op(DVE, lambda h=h, ret=ret, ss=ss: V.tensor_scalar(out=ret[:, h * 128:(h + 1) * 128],
                                                                       in0=ret[:, h * 128:(h + 1) * 128],
                                                                       scalar1=ss[:, 2 * h:2 * h + 1], scalar2=ss[:, 12 + h:13 + h],
                                                                       op0=ALU.subtract, op1=ALU.mult),
                      reads=[bret, bss], writes=[bret])
            fw.op(DVE, lambda ret=ret: V.tensor_tensor(out=ret[:], in0=ret[:], in1=C("gng"), op=ALU.mult),
                  reads=[bret, b_cst], writes=[bret])
            fw.op(DVE, lambda ret=ret: V.tensor_tensor(out=ret[:], in0=ret[:], in1=C("gnb"), op=ALU.add),
                  reads=[bret, b_cst], writes=[bret])
            rob, brob = tb16.next()
            fw.op(DVE, lambda ret=ret, rob=rob, j=j: V.tensor_tensor(out=rob[:], in0=ret[:], in1=sgate[:, j, :], op=ALU.mult),
                  reads=[bret, b_sg[j]], writes=[brob])
            ptt, bpt = tbanks.next()
            fw.group(PE, [(lambda h=h: T.transpose(out=ptt[:, h * 128:(h + 1) * 128], in_=rob[:, h * 128:(h + 1) * 128],
                                                   identity=ident_b[:])) for h in range(4)],
                     reads=[brob, b_ident], writes=[bpt])
            fw.op(ACT, lambda j=j, ptt=ptt: S.activation(out=mixT[:, 0:4, j * 128:(j + 1) * 128],
                                                         in_=ptt[:, 0:512].rearrange("p (h t) -> p h t", h=4), func=AF.Identity),
                  reads=[bpt], writes=[b_mixT[j]])
        dww = C("dww").rearrange("p (c k) -> p c k", c=4)
        if is0:
            for c in range(4):
                pb, bpb = banks.next()
                fw.op(PE, lambda c=c, pb=pb: T.transpose(out=pb[:, 0:120], in_=cache_tm[:, c * 128:(c + 1) * 128],
                                                         identity=ident_f[0:120, 0:120]),
                      reads=[b_cache_tm, b_ident], writes=[bpb])
                fw.op(DVE, lambda c=c, pb=pb: V.tensor_copy(out=uxs[:, c, :, 0:30],
                                                            in_=pb[:, 0:120].rearrange("p (s t) -> p s t", s=4)),
                      reads=[bpb], writes=[b_uxs])
                fw.op(DVE, lambda c=c: V.tensor_copy(out=uxs[:, c, :, 30:46],
                                                     in_=uT[:, c, 30 + 64:30 + 128].rearrange("p (s t) -> p s t", s=4)),
                      reads=[b_uT[c]], writes=[b_uxs])
            fw.op(ACT, lambda: S.activation(out=uxb[:], in_=uxs[:], func=AF.Identity), reads=[b_uxs], writes=[b_uxb])
            for c in range(4):
                pcv, bpcv = banks.next()
                pcv3 = pcv[:, 0:64].rearrange("p (s t) -> p s t", s=4)
                for gi, (dgs, bdgs, k0, k1) in enumerate(((dgA, b_qTm, 0, 16), (dgB, b_ktm, 16, 31))):
                    fw.dma(SP, dgs[:, 0:k1 - k0, :], dg_scr[2 * c + gi, :, 0:(k1 - k0) * 128].rearrange("p (a t) -> p a t", t=128),
                           reads=[b_dgscr], writes=[bdgs])
                    fw.group(PE, [(lambda k=k, dgs=dgs, k0=k0: T.matmul(pcv3, lhsT=dgs[:, k - k0, :], rhs=uxb[:, c, :, k:k + 16],
                                                                        start=(k == 0), stop=(k == 30))) for k in range(k0, k1)],
                             reads=[bdgs, b_uxb], writes=[bpcv])
                fw.op(DVE, lambda c=c: V.memset(yc[:, c, 0:64], 0.0), writes=[b_yc[c]])
                fw.op(ACT, lambda c=c, pcv=pcv: S.activation(out=yc[:, c, 64:128], in_=pcv[:, 0:64], func=AF.Identity,
                                                             bias=C("dwb")[:, c:c + 1]),
                      reads=[bpcv, b_cst, b_yc[c]], writes=[b_yc[c]])
        if is0:
            pb, bpb = banks.next()
            fw.group(PE, [(lambda c=c: T.transpose(out=pb[0:64, c * 128:(c + 1) * 128], in_=uT[:, c, 30 + 64:30 + 128],
                                                   identity=ident_f[:])) for c in range(4)],
                     reads=b_uT + [b_ident], writes=[bpb])
            utm, butm = t32.next()
            fw.op(DVE, lambda pb=pb, utm=utm: V.tensor_copy(out=utm[0:64, :], in_=pb[0:64, :]), reads=[bpb], writes=[butm])
            for s in range(4):
                fw.dma(POOL, co_v[s, 14:30, :], utm[16 * s:16 * s + 16, :], reads=[butm], is_output=True)
        pmean, bpmean = banks.next()
        pex2, bpex2 = banks.next()
        fw.group(PE, [(lambda c=c: T.matmul(pmean[:, 0:N], lhsT=ones_f[:], rhs=yc[:, c, 0:N], start=(c == 0), stop=(c == 3)))
                      for c in range(4)], reads=b_yc + [b_ident], writes=[bpmean])
        ysq_list = []
        for c in range(4):
            ysq, bysq = t32.next()
            fw.op(ACT, lambda c=c, ysq=ysq: S.activation(out=ysq[:, 0:N], in_=yc[:, c, 0:N], func=AF.Square),
                  reads=[b_yc[c]], writes=[bysq])
            ysq_list.append((ysq, bysq))
        fw.group(PE, [(lambda c=c: T.matmul(pex2[:, 0:N], lhsT=ones_f[:], rhs=ysq_list[c][0][:, 0:N], start=(c == 0), stop=(c == 3)))
                      for c in range(4)], reads=[b for _, b in ysq_list] + [b_ident], writes=[bpex2])
        mean_sb, bmean = t32.next()
        rstd_sb, brstd = t32.next()
        fw.op(ACT, lambda: S.activation(out=mean_sb[:, 0:N], in_=pmean[:, 0:N], func=AF.Identity), reads=[bpmean], writes=[bmean])
        fw.op(DVE, lambda: V.tensor_tensor(out=rstd_sb[:, 0:N], in0=mean_sb[:, 0:N], in1=mean_sb[:, 0:N], op=ALU.mult),
              reads=[bmean], writes=[brstd])
        fw.op(DVE, lambda: V.tensor_tensor(out=rstd_sb[:, 0:N], in0=pex2[:, 0:N], in1=rstd_sb[:, 0:N], op=ALU.subtract),
              reads=[bpex2, brstd], writes=[brstd])
        fw.op(ACT, lambda: S.activation(out=rstd_sb[:, 0:N], in_=rstd_sb[:, 0:N], func=AF.Sqrt, bias=EPS),
              reads=[brstd], writes=[brstd])
        fw.op(DVE, lambda: V.reciprocal(out=rstd_sb[:, 0:N], in_=rstd_sb[:, 0:N]), reads=[brstd], writes=[brstd])
        for c in range(4):
            fw.op(DVE, lambda c=c: V.tensor_tensor(out=yc[:, c, 0:N], in0=yc[:, c, 0:N], in1=mean_sb[:, 0:N], op=ALU.subtract),
                  reads=[b_yc[c], bmean], writes=[b_yc[c]])
            fw.op(DVE, lambda c=c: V.tensor_tensor(out=yc[:, c, 0:N], in0=yc[:, c, 0:N], in1=rstd_sb[:, 0:N], op=ALU.mult),
                  reads=[b_yc[c], brstd], writes=[b_yc[c]])
            fw.op(ACT, lambda c=c: S.activation(out=mixT[:, 4 + c, 0:N], in_=yc[:, c, 0:N], func=AF.Silu,
                                                scale=C("clg")[:, c:c + 1], bias=C("clb")[:, c:c + 1]),
                  reads=[b_yc[c], b_cst], writes=[b_mixTc])
        if is0:
            src_lo = 30 + (64 - 30)
            for c in range(4):
                fw.op(POOL, lambda c=c, src_lo=src_lo: G.tensor_copy(out=uhist[:, c, :], in_=uT[:, c, src_lo:src_lo + 30]),
                      reads=[b_uT[c]], writes=[b_uhist])
        fw.handoff(R1_BUFS, R2_BUFS)
        wo0, bwo0 = wload("out", 0, 0)
        wo1, bwo1 = wload("out", 0, 1)
        def post_norm_residual(gname):
            sss = []
            for j in range(nb):
                ss, bss = sst.next()
                sss.append((ss, bss))
                fw.op(ACT, lambda ss=ss, j=j: S.activation(out=junk[:], in_=f_sb[:, j, :], func=AF.Square, accum_out=ss[:, 0:1]),
                      reads=[b_f[j]], writes=[b_junk, bss])
                fw.op(ACT, lambda ss=ss: S.activation(out=ss[:, 1:2], in_=ss[:, 0:1], func=AF.Sqrt, scale=1.0 / D, bias=EPS),
                      reads=[bss], writes=[bss])
            for j in range(nb):
                ss, bss = sss[j]
                fw.op(DVE, lambda ss=ss: V.reciprocal(out=ss[:, 2:3], in_=ss[:, 1:2]), reads=[bss], writes=[bss])
                fw.op(DVE, lambda ss=ss, j=j: V.scalar_tensor_tensor(out=f_sb[:, j, :], in0=f_sb[:, j, :], scalar=ss[:, 2:3], in1=C(gname),
                                                                     op0=ALU.mult, op1=ALU.mult),
                      reads=[b_f[j], bss, b_cst], writes=[b_f[j]])
                fw.op(DVE, lambda j=j: V.tensor_tensor(out=x1[j], in0=x1[j], in1=f_sb[:, j, :], op=ALU.add),
                      reads=[b_f[j], b_x1[j]], writes=[b_x1[j]])

        P5 = {}

        def s5_0(j):
            for hf, (wo, bwo) in enumerate(((wo0, bwo0), (wo1, bwo1))):
                pm, bpm = banks.next()
                fw.group(PE, [(lambda k=k, pm=pm, wo=wo: T.matmul(pm, lhsT=mixT[:, k, j * 128:(j + 1) * 128], rhs=wo[:, k, :],
                                                                  start=(k == 0), stop=(k == 7))) for k in range(8)],
                         reads=[b_mixT[j], b_mixTc, bwo], writes=[bpm])
                fw.op(ACT, lambda hf=hf, pm=pm: S.activation(out=f_sb[:, j, hf * 512:(hf + 1) * 512], in_=pm, func=AF.Identity),
                      reads=[bpm], writes=[b_f[j]])

        def s5_1(j):
            ss, bss = sst.next()
            P5[j] = dict(ss=ss, bss=bss)
            fw.op(ACT, lambda: S.activation(out=junk[:], in_=f_sb[:, j, :], func=AF.Square, accum_out=ss[:, 0:1]),
                  reads=[b_f[j]], writes=[b_junk, bss])
            fw.op(ACT, lambda: S.activation(out=ss[:, 1:2], in_=ss[:, 0:1], func=AF.Sqrt, scale=1.0 / D, bias=EPS), reads=[bss], writes=[bss])

        def s5_2(j):
            ss, bss = P5[j]["ss"], P5[j]["bss"]
            fw.op(DVE, lambda: V.reciprocal(out=ss[:, 2:3], in_=ss[:, 1:2]), reads=[bss], writes=[bss])
            fw.op(DVE, lambda: V.scalar_tensor_tensor(out=f_sb[:, j, :], in0=f_sb[:, j, :], scalar=ss[:, 2:3], in1=C("gpm"),
                                                      op0=ALU.mult, op1=ALU.mult), reads=[b_f[j], bss, b_cst], writes=[b_f[j]])
            fw.op(DVE, lambda: V.tensor_tensor(out=x1[j], in0=x1[j], in1=f_sb[:, j, :], op=ALU.add),
                  reads=[b_f[j], b_x1[j]], writes=[b_x1[j]])

        def s5_3(j):
            ss, bss = sst.next()
            P5[j].update(ss2=ss, bss2=bss)
            fw.op(ACT, lambda: S.activation(out=junk[:], in_=x1[j], func=AF.Square, accum_out=ss[:, 0:1]),
                  reads=[b_x1[j]], writes=[b_junk, bss])
            fw.op(ACT, lambda: S.activation(out=ss[:, 1:2], in_=ss[:, 0:1], func=AF.Sqrt, scale=1.0 / D, bias=EPS), reads=[bss], writes=[bss])

        def s5_4(j):
            ss, bss = P5[j]["ss2"], P5[j]["bss2"]
            xbt = aT[:, 2 * j:2 * j + 2, :].rearrange("p k n -> p (k n)")
            fw.op(DVE, lambda: V.reciprocal(out=ss[:, 2:3], in_=ss[:, 1:2]), reads=[bss], writes=[bss])
            fw.op(DVE, lambda: V.scalar_tensor_tensor(out=xbt, in0=x1[j], scalar=ss[:, 2:3], in1=C("gmlb"), op0=ALU.mult, op1=ALU.mult),
                  reads=[b_x1[j], bss, b_cst], writes=[b_aT[2 * j], b_aT[2 * j + 1]])

        def s5_5(j):
            xbt = aT[:, 2 * j:2 * j + 2, :].rearrange("p k n -> p (k n)")
            norm_part2([(None, None, xbt, [b_aT[2 * j], b_aT[2 * j + 1]], h2T[:, :, j * 128:(j + 1) * 128], [b_h2T[j]])])

        s5 = [s5_0, s5_1, s5_2, s5_3, s5_4, s5_5]
        for t in range(nb + len(s5) - 1):
            for si in reversed(range(len(s5))):
                j = t - si
                if 0 <= j < nb:
                    s5[si](j)
        for cgi in range(8):
            wt, bw = wload("mi", 0, cgi)
            for c in range(4):
                pb, bpb = banks.next()
                fw.group(PE, [(lambda k=k: T.matmul(pb[:, 0:N], lhsT=wt[:, k, c * 128:(c + 1) * 128], rhs=h2T[:, k, 0:N],
                                                    start=(k == 0), stop=(k == 7))) for k in range(8)],
                         reads=b_h2T[0:nb] + [bw], writes=[bpb])
                rl, brl = t32.next()
                ai = cgi * 4 + c
                fw.op(ACT, lambda pb=pb, rl=rl: S.activation(out=rl[:, 0:N], in_=pb[:, 0:N], func=AF.Relu), reads=[bpb], writes=[brl])
                fw.op(DVE, lambda ai=ai, rl=rl: V.tensor_tensor(out=aT[:, ai, 0:N], in0=rl[:, 0:N], in1=rl[:, 0:N], op=ALU.mult),
                      reads=[brl], writes=[b_aT[ai]])
        nxt = None
        for hf in range(2):
            accs = [banks.next() for _ in range(nb)]
            for kg in range(4):
                wt, bw = wload("mo", kg, hf)
                for j in range(nb):
                    pacc, bpacc = accs[j]
                    fw.group(PE, [(lambda k=k, pacc=pacc: T.matmul(pacc, lhsT=aT[:, kg * 8 + k, j * 128:(j + 1) * 128], rhs=wt[:, k, :],
                                                                   start=(kg == 0 and k == 0), stop=(kg == 3 and k == 7)))
                                  for k in range(8)],
                             reads=b_aT[kg * 8:(kg + 1) * 8] + [bw], writes=[bpacc])
            for j in range(nb):
                pacc, bpacc = accs[j]
                if j % 2 == 0:
                    fw.op(ACT, lambda j=j, pacc=pacc, hf=hf: S.activation(out=f_sb[:, j, hf * 512:(hf + 1) * 512], in_=pacc,
                                                                          func=AF.Identity), reads=[bpacc], writes=[b_f[j]])
                else:
                    fw.op(DVE, lambda j=j, pacc=pacc, hf=hf: V.tensor_copy(out=f_sb[:, j, hf * 512:(hf + 1) * 512], in_=pacc),
                          reads=[bpacc], writes=[b_f[j]])
            if hf == 0 and sti + 1 < len(ST_BLOCKS):
                nxt = stage0_part1(sti + 1)
        if nxt is not None:
            stage0_part2(nxt)
        post_norm_residual("gpo")
        for j, blk in enumerate(blks):
            fw.dma(SP, y_all[blk * 128:(blk + 1) * 128, :], x1[j], reads=[b_x1[j]], is_output=True)
    fw.dma(POOL, o_sfin[:, :], S32[:], reads=[b_S32], is_output=True)
    fw.finish()
    nc.all_engine_barrier()
    for sem in fw.all_sems():
        nc.gpsimd.sem_clear(sem)
    nc.all_engine_barrier()
    return nc


def _core_consts(c, g_pre_mix, g_pre_mlp, g_post_mix, g_post_mlp, gn_g, gn_b, dw_w, dw_b, cln_g, cln_b):
    f64 = np.float64
    gam = np.array(GAMMA, f64)
    tab = np.zeros((128, CTOT), np.float32)

    def put(name, arr):
        o, s = COFF[name]
        tab[:, o:o + s] = np.asarray(arr, np.float32).reshape(128, s)

    r = np.arange(128)
    pos = np.zeros((128, NBLK), f64)
    il = np.zeros((128, NBLK), f64)
    L = np.full((128, NBLK), 64.0)
    if c == 0:
        pos[:64, 0] = np.maximum(r[:64] - 48, 0)
    else:
        pos[:64, 0] = 16 + 2048 * c - 64 + r[:64]
    pos[64:, 0] = 16 + 4096 + (r[64:] - 64) % 16
    il[:64, 0] = r[:64]
    il[64:, 0] = (r[64:] - 64) % 16
    L[64:, 0] = 16
    for b in range(1, NBLK):
        pos[:, b] = 16 + 2048 * c + (b - 1) * 128 + r
        il[:, b] = r % 64
    inv = 10000.0 ** (-np.arange(64, dtype=f64) / 64.0)
    ang = pos[:, :, None] * inv[None, None, :]
    put("cos", np.cos(ang))
    put("sin", np.sin(ang))
    put("dq", gam[None, None, :] ** (il[:, :, None] + 1.0))
    dkB = SCALE * gam[None, None, :] ** (L[:, :, None] - 1.0 - il[:, :, None])
    put("dkB", dkB)
    jj, ii = np.meshgrid(r, r, indexing="ij")
    same = (jj // 64) == (ii // 64)
    dstd = np.where(same[:, None, :], SCALE * gam[None, :, None] ** np.abs(ii - jj)[:, None, :], 0.0)
    put("dstd", dstd)
    same0 = np.where((jj < 64) | (ii < 64), (jj < 64) & (ii < 64), (jj // 16) == (ii // 16))
    db0 = np.where(same0[:, None, :], SCALE * gam[None, :, None] ** np.abs(ii - jj)[:, None, :], 0.0)
    put("db0", db0)
    seqm = np.zeros((128, 4))
    for s in range(4):
        seqm[64 + 16 * s: 64 + 16 * s + 16, s] = 1.0
    put("seqm", seqm)
    put("gpre", g_pre_mix.reshape(8, 128).T)
    put("gprb", np.broadcast_to(g_pre_mix, (128, 1024)))
    put("gmlb", np.broadcast_to(g_pre_mlp, (128, 1024)))
    put("gpm", np.broadcast_to(g_post_mix, (128, 1024)))
    put("gpo", np.broadcast_to(g_post_mlp, (128, 1024)))
    put("gng", np.broadcast_to(gn_g, (128, 512)))
    put("gnb", np.broadcast_to(gn_b, (128, 512)))
    put("dww", dw_w.reshape(31, 4, 128).transpose(2, 1, 0))
    put("dwb", dw_b.reshape(4, 128).T)
    put("clg", cln_g.reshape(4, 128).T)
    put("clb", cln_b.reshape(4, 128).T)
    put("g64t", np.broadcast_to(np.repeat(np.array(G64), 128), (128, 512)))
    p = 64 + 2048 * c - NPRE * 128 + (np.arange(NPRE)[None, :] * 128 + r[:, None])
    posA = np.maximum(p - 48, 0).astype(f64)
    angA = posA[:, :, None] * inv[None, None, :]
    dist = (64 + 2048 * c - 1 - p).astype(f64)
    dkA = SCALE * gam[None, None, :] ** dist[:, :, None]
    tabA = np.concatenate([np.cos(angA).reshape(128, -1), np.sin(angA).reshape(128, -1), dkA.reshape(128, -1)], 1)
    return tab, np.ascontiguousarray(tabA, dtype=np.float32)


_NC_CACHE = {}


def kernel(x_prompt, x_sample, state_ret, cache_conv, meta, g_pre_mix, w_in, gn_g, gn_b, dw_w, dw_b,
           cln_g, cln_b, w_out, g_post_mix, g_pre_mlp, w_mlp_in, w_mlp_out, g_post_mlp):
    f = lambda a: np.ascontiguousarray(np.asarray(a, dtype=np.float32))
    x_prompt, x_sample, state_ret, cache_conv, meta = map(f, (x_prompt, x_sample, state_ret, cache_conv, meta))
    w_in0, w_out0, w_mi0, w_mo0 = f(w_in)[0], f(w_out)[0], f(w_mlp_in)[0], f(w_mlp_out)[0]
    vecs = [f(v)[0] for v in (g_pre_mix, g_pre_mlp, g_post_mix, g_post_mlp, gn_g, gn_b, dw_w, dw_b, cln_g, cln_b)]
    xp = x_prompt[0]
    if "nc" not in _NC_CACHE:
        _NC_CACHE["nc"] = build_program()
    nc = _NC_CACHE["nc"]
    in_maps = []
    xpad = np.concatenate([np.zeros((48, D), np.float32), meta, xp], 0)
    for c in range(NCORES):
        xa = np.zeros((NTOK, D), np.float32)
        if c == 0:
            xa[48:64] = meta
        else:
            xa[0:64] = xp[2048 * c - 64: 2048 * c]
        xa[64:128] = x_sample[4 * c: 4 * c + 4].reshape(64, D)
        xa[128:] = xp[2048 * c: 2048 * (c + 1)]
        tab, tabA = _core_consts(c, *vecs)
        lo = 64 + 2048 * c - NPRE * 128
        xpre = np.zeros((NPRE * 128, D), np.float32)
        src_lo = max(lo, 0)
        xpre[src_lo - lo:] = xpad[src_lo: 64 + 2048 * c]
        in_maps.append({
            "x_all": xa, "x_pre": xpre, "cstA": tabA, "w_in": w_in0, "w_out": w_out0, "w_mi": w_mi0, "w_mo": w_mo0,
            "state_s": np.ascontiguousarray(state_ret[0, 4 * c: 4 * c + 4].reshape(16 * 128, 128)),
            "cache_s": np.ascontiguousarray(cache_conv[0, 4 * c: 4 * c + 4].reshape(120, 512)),
            "cst": tab,
        })
    res = run_bass_kernel_spmd(nc, in_maps, core_ids=list(range(NCORES)))
    R = res.results
    y_prompt = np.concatenate([R[c]["y_all"][128:] for c in range(NCORES)], 0)[None]
    y_sample = np.concatenate([R[c]["y_all"][64:128].reshape(4, 16, D) for c in range(NCORES)], 0)
    s_p = R[NCORES - 1]["o_sfin"].reshape(128, 4, 128).transpose(1, 0, 2)[None, None]
    conv_p = R[NCORES - 1]["o_convp"][None, None]
    s_s = np.concatenate([R[c]["o_ss"].reshape(128, 4, 4, 128).transpose(1, 2, 0, 3) for c in range(NCORES)], 0)[None]
    conv_s = np.concatenate([R[c]["o_convs"].reshape(4, 30, 512) for c in range(NCORES)], 0)[None]
    return (np.ascontiguousarray(y_prompt, dtype=np.float32), np.ascontiguousarray(y_sample, dtype=np.float32),
            np.ascontiguousarray(s_p, dtype=np.float32), np.ascontiguousarray(conv_p, dtype=np.float32),
            np.ascontiguousarray(s_s, dtype=np.float32), np.ascontiguousarray(conv_s, dtype=np.float32))
```
